# Optimizing a Trainium2 kernel written in Bass

```python
import math
import jax, jax.numpy as jnp
from jax import lax
import numpy as np

D_MODEL = 2048
BATCH = 4
SEQ = 2048
DEPTH = 4
DEC_BATCH = 2
DEC_SEQ = 4096
PAST_LEN = 128

HEAD_DIM = 128
ATT_HEADS = (D_MODEL // 2) // HEAD_DIM
ATT_WIDTH = ATT_HEADS * HEAD_DIM
ROPE_DIM = HEAD_DIM // 4
ROPE_THETA = 500000.0
DILATED_BRANCHES = ((128, 1), (512, 4), (2048, 16))
SSM_HEAD_DIM = 64
SSM_WIDTH = D_MODEL // 2
SSM_HEADS = SSM_WIDTH // SSM_HEAD_DIM
SSM_GROUPS = 2
SSM_HEADS_PER_GROUP = SSM_HEADS // SSM_GROUPS
SSM_STATE = 128
SSM_CONV = 4
SSM_CHUNK = 128
SSM_CONV_CH = SSM_WIDTH + 2 * SSM_GROUPS * SSM_STATE
IN_COLS = 3 * ATT_WIDTH + SSM_WIDTH + SSM_CONV_CH + 2 * SSM_HEADS
CONF_CH = D_MODEL
CONF_KERNEL = 31
FFN_HIDDEN = -(-8 * D_MODEL // (3 * 256)) * 256
N_EVEN = (DEPTH + 1) // 2
N_ODD = DEPTH // 2
EPS = 1e-6
NEG_INF = -1e30

kernel_name = 'hybrid_dilated_ssd_conformer_encoder'


def rmsnorm(x, g):
    xf = x.astype(jnp.float32)
    y = xf * lax.rsqrt(jnp.mean(xf * xf, axis=-1, keepdims=True) + EPS)
    return (y * g.astype(jnp.float32)).astype(x.dtype)


def layernorm(x, g, b):
    xf = x.astype(jnp.float32)
    mu = jnp.mean(xf, axis=-1, keepdims=True)
    xc = xf - mu
    y = xc * lax.rsqrt(jnp.mean(xc * xc, axis=-1, keepdims=True) + EPS)
    return (y * g.astype(jnp.float32) + b.astype(jnp.float32)).astype(x.dtype)


def depthwise_conv(x, w, b):
    width = w.shape[0]
    left = (width - 1) // 2
    y = lax.conv_general_dilated(x, w[:, None, :].astype(x.dtype), window_strides=(1,),
                                 padding=[(left, width - 1 - left)],
                                 dimension_numbers=('NWC', 'WIO', 'NWC'),
                                 feature_group_count=x.shape[-1])
    return y + b.astype(y.dtype)


def partial_rope(t, cos, sin):
    rot, rest = t[..., :ROPE_DIM], t[..., ROPE_DIM:]
    r1, r2 = rot[..., :ROPE_DIM // 2], rot[..., ROPE_DIM // 2:]
    rot = jnp.concatenate([r1 * cos - r2 * sin, r2 * cos + r1 * sin], axis=-1)
    return jnp.concatenate([rot.astype(t.dtype), rest], axis=-1)


def to_residue_classes(t, dil):
    b, s = t.shape[:2]
    return t.reshape(b, s // dil, dil, *t.shape[2:]).swapaxes(1, 2).reshape(b * dil, s // dil, *t.shape[2:])


def from_residue_classes(t, batch, dil):
    l = t.shape[1]
    return t.reshape(batch, dil, l, *t.shape[2:]).swapaxes(1, 2).reshape(batch, dil * l, *t.shape[2:])


def dilated_branch(q, k, v, window, dil):
    batch = q.shape[0]
    half = window // (2 * dil)
    blk = half
    qc, kc, vc = (to_residue_classes(t, dil) for t in (q, k, v))
    bb, L = qc.shape[0], qc.shape[1]
    nb = -(-L // blk)
    lp = nb * blk
    qb = jnp.pad(qc, ((0, 0), (0, lp - L), (0, 0), (0, 0))).reshape(bb, nb, blk, ATT_HEADS, HEAD_DIM)

    def windows(t):
        tp = jnp.pad(t, ((0, 0), (blk, blk + lp - L), (0, 0), (0, 0))).reshape(bb, nb + 2, blk, ATT_HEADS, HEAD_DIM)
        return jnp.concatenate([tp[:, :-2], tp[:, 1:-1], tp[:, 2:]], axis=2)

    kw, vw = windows(kc), windows(vc)
    s = jnp.einsum('bnqhd,bnkhd->bnhqk', qb.astype(jnp.float32), kw.astype(jnp.float32)) * (HEAD_DIM ** -0.5)
    n = jnp.arange(nb)[:, None, None]
    qpos = n * blk + jnp.arange(blk)[None, :, None]
    kpos = (n - 1) * blk + jnp.arange(3 * blk)[None, None, :]
    allowed = (jnp.abs(qpos - kpos) <= half) & (kpos >= 0) & (kpos < L)
    s = jnp.where(allowed[None, :, None], s, NEG_INF)
    m = jnp.max(s, axis=-1)
    p = jnp.exp(s - m[..., None])
    l = jnp.sum(p, axis=-1)
    o = jnp.einsum('bnhqk,bnkhd->bnqhd', p, vw.astype(jnp.float32))

    def back(t):
        t = t.reshape(bb, lp, *t.shape[3:])[:, :L]
        return from_residue_classes(t, batch, dil)

    return back(o), back(m.swapaxes(2, 3)), back(l.swapaxes(2, 3))


def dilated_attention(q, k, v, q_norm, k_norm):
    bn, S, _ = q.shape
    q = rmsnorm(q.reshape(bn, S, ATT_HEADS, HEAD_DIM), q_norm)
    k = rmsnorm(k.reshape(bn, S, ATT_HEADS, HEAD_DIM), k_norm)
    v = v.reshape(bn, S, ATT_HEADS, HEAD_DIM)
    pos = jnp.arange(S, dtype=jnp.float32)
    inv_freq = ROPE_THETA ** (-jnp.arange(0, ROPE_DIM, 2, dtype=jnp.float32) / ROPE_DIM)
    ang = pos[:, None] * inv_freq[None, :]
    cos, sin = jnp.cos(ang)[:, None, :], jnp.sin(ang)[:, None, :]
    q = partial_rope(q, cos, sin)
    k = partial_rope(k, cos, sin)
    outs, maxes, dens = zip(*[dilated_branch(q, k, v, w, d) for (w, d) in DILATED_BRANCHES])
    maxes = jnp.stack(maxes)
    wts = jnp.exp(maxes - jnp.max(maxes, axis=0))
    num = jnp.einsum('gbsh,gbshd->bshd', wts, jnp.stack(outs))
    den = jnp.sum(wts * jnp.stack(dens), axis=0)
    out = num / den[..., None]
    return out.reshape(bn, S, ATT_WIDTH).astype(q.dtype)


def ssd_chunked(x, delta, a, bm, cm):
    bn, S, G, J, P = x.shape
    N = bm.shape[-1]
    T = SSM_CHUNK
    c = S // T
    X = (x * delta[..., None]).reshape(bn, c, T, G, J, P)
    adt = (delta * a).reshape(bn, c, T, G, J).transpose(0, 1, 3, 4, 2)
    Bc = bm.reshape(bn, c, T, G, N)
    Cc = cm.reshape(bn, c, T, G, N)
    a_cum = jnp.cumsum(adt, axis=-1)
    seg = a_cum[..., :, None] - a_cum[..., None, :]
    lower = jnp.tril(jnp.ones((T, T), dtype=bool))
    lmat = jnp.exp(jnp.where(lower, seg, -jnp.inf))
    cb = jnp.einsum('bclgn,bcsgn->bcgls', Cc, Bc)
    y_diag = jnp.einsum('bcgjls,bcsgjp->bclgjp', cb[:, :, :, None] * lmat, X)
    decay_states = jnp.exp(a_cum[..., -1:] - a_cum).transpose(0, 1, 4, 2, 3)
    states = jnp.einsum('bcsgn,bcsgjp->bcgjpn', Bc, X * decay_states[..., None])
    chunk_decay = jnp.exp(a_cum[..., -1])

    def step(h, inp):
        st, dec = inp
        return h * dec[..., None, None] + st, h

    h0 = jnp.zeros((bn, G, J, P, N), jnp.float32)
    _, prev = lax.scan(step, h0, (jnp.moveaxis(states, 1, 0), jnp.moveaxis(chunk_decay, 1, 0)))
    prev = jnp.moveaxis(prev, 0, 1)
    y_off = jnp.einsum('bclgn,bcgjpn->bclgjp', Cc, prev) * jnp.exp(a_cum).transpose(0, 1, 4, 2, 3)[..., None]
    return (y_diag + y_off).reshape(bn, S, G, J, P)


def bidirectional_ssd(z, xbc, dt, conv_w, conv_b, a_log, dt_bias, d_skip, ssm_norm):
    bn, S, _ = z.shape
    G, J, P, N = SSM_GROUPS, SSM_HEADS_PER_GROUP, SSM_HEAD_DIM, SSM_STATE
    xbc = jax.nn.silu(depthwise_conv(xbc, conv_w, conv_b)).astype(jnp.float32)
    xs, bm, cm = jnp.split(xbc, [SSM_WIDTH, SSM_WIDTH + G * N], axis=-1)
    xs = xs.reshape(bn, S, G, J, P)
    bm = bm.reshape(bn, S, G, N)
    cm = cm.reshape(bn, S, G, N)
    delta = jax.nn.softplus(dt.astype(jnp.float32).reshape(bn, S, 2, G, J)
                            + dt_bias.astype(jnp.float32).reshape(2, G, J))
    a = -jnp.exp(a_log.astype(jnp.float32)).reshape(2, G, J)
    y_fwd = ssd_chunked(xs, delta[:, :, 0], a[0], bm, cm)
    flip = lambda t: t[:, ::-1]
    y_bwd = flip(ssd_chunked(flip(xs), flip(delta[:, :, 1]), a[1], flip(bm), flip(cm)))
    y = y_fwd + y_bwd + xs * d_skip.astype(jnp.float32).reshape(G, J, 1)
    y = y.reshape(bn, S, SSM_WIDTH) * jax.nn.silu(z.astype(jnp.float32))
    y = y.reshape(bn, S, G, SSM_WIDTH // G)
    y = y * lax.rsqrt(jnp.mean(y * y, axis=-1, keepdims=True) + EPS)
    y = y.reshape(bn, S, SSM_WIDTH) * ssm_norm.astype(jnp.float32)
    return y.astype(z.dtype)


def hybrid_mixer(h, w_in, q_norm, k_norm, conv_w, conv_b, a_log, dt_bias, d_skip, ssm_norm, w_out):
    proj = h @ w_in
    cuts = np.cumsum([ATT_WIDTH, ATT_WIDTH, ATT_WIDTH, SSM_WIDTH, SSM_CONV_CH]).tolist()
    q, k, v, z, xbc, dt = jnp.split(proj, cuts, axis=-1)
    att = dilated_attention(q, k, v, q_norm, k_norm)
    ssm = bidirectional_ssd(z, xbc, dt, conv_w, conv_b, a_log, dt_bias, d_skip, ssm_norm)
    return jnp.concatenate([att, ssm], axis=-1) @ w_out


def conformer_conv(h, pw1_w, pw1_b, dw_w, dw_b, ln_g, ln_b, pw2_w, pw2_b):
    u = h @ pw1_w + pw1_b
    a, g = jnp.split(u, 2, axis=-1)
    u = a * jax.nn.sigmoid(g)
    u = depthwise_conv(u, dw_w, dw_b)
    u = jax.nn.silu(layernorm(u, ln_g, ln_b))
    return u @ pw2_w + pw2_b


def swiglu(h, w_gate, w_up, w_down):
    return (jax.nn.silu(h @ w_gate) * (h @ w_up)) @ w_down


def encoder_trunk(x, p):
    for layer in range(DEPTH):
        if layer % 2 == 0:
            e = layer // 2
            x = x + hybrid_mixer(rmsnorm(x, p['mix_norm'][e]), p['w_in'][e], p['q_norm'][e], p['k_norm'][e],
                                 p['ssm_conv_w'][e], p['ssm_conv_b'][e], p['a_log'][e], p['dt_bias'][e],
                                 p['d_skip'][e], p['ssm_norm'][e], p['w_out'][e])
        else:
            o = layer // 2
            x = x + conformer_conv(rmsnorm(x, p['conf_norm'][o]), p['pw1_w'][o], p['pw1_b'][o], p['dw_w'][o],
                                   p['dw_b'][o], p['ln_g'][o], p['ln_b'][o], p['pw2_w'][o], p['pw2_b'][o])
        x = x + swiglu(rmsnorm(x, p['ffn_norm'][layer]), p['w_gate'][layer], p['w_up'][layer], p['w_down'][layer])
    return x


def setup_inputs(seed: int = 0) -> dict:
    key = jax.random.key(seed)
    ks = jax.random.split(key, 32)
    f32 = jnp.float32
    nrm = lambda k, shape, scale: jax.random.normal(k, shape, f32) * scale
    gain = lambda k, shape: 1.0 + 0.02 * jax.random.normal(k, shape, f32)
    dt0 = jnp.exp(jax.random.uniform(ks[9], (N_EVEN, 2, SSM_HEADS), f32, math.log(1e-3), math.log(1e-1)))
    return {
        'x_prompt': jax.random.normal(ks[0], (BATCH, SEQ, D_MODEL), f32),
        'x_sample': jax.random.normal(ks[1], (DEC_BATCH, DEC_SEQ, D_MODEL), f32),
        'mix_norm': gain(ks[2], (N_EVEN, D_MODEL)),
        'w_in': nrm(ks[3], (N_EVEN, D_MODEL, IN_COLS), D_MODEL ** -0.5),
        'q_norm': gain(ks[4], (N_EVEN, HEAD_DIM)),
        'k_norm': gain(ks[5], (N_EVEN, HEAD_DIM)),
        'ssm_conv_w': nrm(ks[6], (N_EVEN, SSM_CONV, SSM_CONV_CH), SSM_CONV ** -0.5),
        'ssm_conv_b': nrm(ks[7], (N_EVEN, SSM_CONV_CH), 0.01),
        'a_log': jnp.log(jax.random.uniform(ks[8], (N_EVEN, 2, SSM_HEADS), f32, 1.0, 16.0)),
        'dt_bias': dt0 + jnp.log(-jnp.expm1(-dt0)),
        'd_skip': gain(ks[10], (N_EVEN, SSM_HEADS)),
        'ssm_norm': gain(ks[11], (N_EVEN, SSM_WIDTH)),
        'w_out': nrm(ks[12], (N_EVEN, ATT_WIDTH + SSM_WIDTH, D_MODEL), (ATT_WIDTH + SSM_WIDTH) ** -0.5),
        'conf_norm': gain(ks[13], (N_ODD, D_MODEL)),
        'pw1_w': nrm(ks[14], (N_ODD, D_MODEL, 2 * CONF_CH), D_MODEL ** -0.5),
        'pw1_b': nrm(ks[15], (N_ODD, 2 * CONF_CH), 0.01),
        'dw_w': nrm(ks[16], (N_ODD, CONF_KERNEL, CONF_CH), CONF_KERNEL ** -0.5),
        'dw_b': nrm(ks[17], (N_ODD, CONF_CH), 0.01),
        'ln_g': gain(ks[18], (N_ODD, CONF_CH)),
        'ln_b': nrm(ks[19], (N_ODD, CONF_CH), 0.01),
        'pw2_w': nrm(ks[20], (N_ODD, CONF_CH, D_MODEL), CONF_CH ** -0.5),
        'pw2_b': nrm(ks[21], (N_ODD, D_MODEL), 0.01),
        'ffn_norm': gain(ks[22], (DEPTH, D_MODEL)),
        'w_gate': nrm(ks[23], (DEPTH, D_MODEL, FFN_HIDDEN), D_MODEL ** -0.5),
        'w_up': nrm(ks[24], (DEPTH, D_MODEL, FFN_HIDDEN), D_MODEL ** -0.5),
        'w_down': nrm(ks[25], (DEPTH, FFN_HIDDEN, D_MODEL), FFN_HIDDEN ** -0.5),
    }


def reference(x_prompt, x_sample, mix_norm, w_in, q_norm, k_norm, ssm_conv_w, ssm_conv_b, a_log, dt_bias,
              d_skip, ssm_norm, w_out, conf_norm, pw1_w, pw1_b, dw_w, dw_b, ln_g, ln_b, pw2_w, pw2_b,
              ffn_norm, w_gate, w_up, w_down):
    params = dict(mix_norm=mix_norm, w_in=w_in, q_norm=q_norm, k_norm=k_norm, ssm_conv_w=ssm_conv_w,
                  ssm_conv_b=ssm_conv_b, a_log=a_log, dt_bias=dt_bias, d_skip=d_skip, ssm_norm=ssm_norm,
                  w_out=w_out, conf_norm=conf_norm, pw1_w=pw1_w, pw1_b=pw1_b, dw_w=dw_w, dw_b=dw_b,
                  ln_g=ln_g, ln_b=ln_b, pw2_w=pw2_w, pw2_b=pw2_b, ffn_norm=ffn_norm, w_gate=w_gate,
                  w_up=w_up, w_down=w_down)
    y_prompt = encoder_trunk(x_prompt, params)
    y_sample = encoder_trunk(x_sample, params)
    return (y_prompt, y_sample)
```

```python
import numpy as np
import ml_dtypes
from contextlib import ExitStack
import concourse.bass as bass
import concourse.mybir as mybir
from concourse.bass_utils import run_bass_kernel_spmd

F32 = mybir.dt.float32
BF16 = mybir.dt.bfloat16
AF = mybir.ActivationFunctionType
ALU = mybir.AluOpType

D = 2048
KC = D // 128
T = 4096
NCORES = 4
DEPTH = 4
FH = 5632
FC = FH // 128
IN_COLS = 5664
EPS = 1e-6
NEG = -30000.0


class Buf:
    __slots__ = ("name", "w", "rd", "sem", "cnt", "bg")

    def __init__(self, name):
        self.name = name
        self.w = None
        self.rd = []
        self.sem = None
        self.cnt = 0
        self.bg = False


class Op:
    __slots__ = ("eng", "fn", "deps", "sig", "phase", "dma", "dbuf", "dval", "sval", "dwait")


def _run_len(ap, is_sbuf):
    pairs = list(ap.ap)
    if is_sbuf:
        pairs = pairs[1:]
    run = 1
    for stride, size in reversed(pairs):
        if size == 1:
            continue
        if stride != run:
            break
        run *= size
    return run


class _Rec:
    def dma_start(self, out=None, in_=None, **kw):
        self.out, self.in_ = out, in_
        return self


def _ndesc(fn):
    rec = _Rec()
    fn(rec)
    n = 1
    for d in rec.out.shape:
        n *= d
    runs = [_run_len(a, "sb" in str(a.space).lower()) for a in (rec.out, rec.in_)]
    return max(n // max(min(runs), 1), 1)


class Prog:
    ENGS = ("sp", "act", "dve", "pool", "pe")

    def __init__(self, nc, es):
        self.nc = nc
        self.es = es
        self.ops = []
        self.phase = 0
        self.esem = {e: es.enter_context(nc.semaphore("e_" + e)) for e in self.ENGS}
        self.ecnt = {e: 0 for e in self.ENGS}
        self.waited = {e: {} for e in self.ENGS}
        self.nsem = 0
        self.bufs = {}
        self.dbufs = []
        self.qtot = {"sp": 0, "act": 0, "pool": 0}
        self.pool_ok = False

    def buf(self, *key):
        b = self.bufs.get(key)
        if b is None:
            b = Buf("_".join(str(k) for k in key))
            self.bufs[key] = b
        return b

    def _mk(self, eng, fn, r, w):
        op = Op()
        op.eng = eng
        op.fn = fn
        op.sig = False
        op.phase = self.phase
        op.dma = False
        op.dbuf = None
        op.dval = 0
        op.sval = 0
        op.dwait = {}
        raw = []
        war = []
        for b in r:
            if b.w is not None:
                raw.append(b.w)
        for b in w:
            if b.w is not None:
                raw.append(b.w)
            for d in b.rd:
                war.append(d)
        keep = []
        seen = set()
        for lst, is_war in ((raw, False), (war, True)):
            for d in lst:
                if id(d) in seen or d is op:
                    continue
                seen.add(id(d))
                if d.dma:
                    k = id(d.dbuf)
                    v = d.dbuf.cnt
                    if op.dwait.get(k, (None, 0))[1] < v:
                        op.dwait[k] = (d.dbuf.sem, v)
                else:
                    if d.phase != self.phase:
                        continue
                    if d.eng == eng and (eng == "pe" or is_war):
                        continue
                    keep.append(d)
        op.deps = keep
        for b in r:
            if not self._is_dma_next:
                b.rd = [d for d in b.rd if d.dma or d.eng != eng]
            b.rd.append(op)
        for b in w:
            b.w = op
            b.rd = []
        self.ops.append(op)
        return op

    def op(self, eng, fn, r=(), w=()):
        self._is_dma_next = False
        return self._mk(eng, fn, r, w)

    def dma(self, eng, fn, r=(), w=(), key=None):
        nd = _ndesc(fn)
        if eng == "sp" and self.pool_ok and self.qtot["pool"] < self.qtot["sp"]:
            eng = "pool"
        self.qtot[eng] += nd
        self._is_dma_next = True
        op = self._mk(eng, fn, r, w)
        op.dma = True
        if key.sem is None:
            key.sem = self.es.enter_context(self.nc.semaphore("d%d" % self.nsem))
            self.nsem += 1
            self.dbufs.append(key)
        key.cnt += 16
        op.dbuf = key
        op.dval = key.cnt
        return op

    def flush(self, name):
        nc = self.nc
        ops = self.ops
        self.ops = []
        for op in ops:
            for d in op.deps:
                d.sig = True
        per = {e: [] for e in self.ENGS}
        for op in ops:
            per[op.eng].append(op)
        for e in self.ENGS:
            c = self.ecnt[e]
            for op in per[e]:
                if op.dma:
                    continue
                if op.sig:
                    c += 1
                    op.sval = c
            self.ecnt[e] = c
        esem = self.esem
        waited = self.waited

        def run(e, eng_obj):
            wd = waited[e]
            for op in per[e]:
                need = {}
                for d in op.deps:
                    s = esem[d.eng]
                    k = ("e", d.eng)
                    if need.get(k, (None, 0))[1] < d.sval:
                        need[k] = (s, d.sval)
                for k, (s, v) in op.dwait.items():
                    need[("d", k)] = (s, v)
                for k, (s, v) in need.items():
                    if wd.get(k, 0) < v:
                        eng_obj.wait_ge(s, v)
                        wd[k] = v
                inst = op.fn(eng_obj)
                if op.dma:
                    inst.then_inc(op.dbuf.sem, 16)
                elif op.sig:
                    inst.then_inc(esem[e], 1)

        dbufs = self.dbufs

        def drain(eng_obj):
            wd = waited["sp"]
            for b in dbufs:
                if b.bg or b.cnt == 0:
                    continue
                k = ("d", id(b))
                if wd.get(k, 0) < b.cnt:
                    eng_obj.wait_ge(b.sem, b.cnt)
                    wd[k] = b.cnt

        with nc.Block() as block:
            @block.sync
            def _(x):
                run("sp", x)
                drain(x)
            if per["act"]:
                @block.scalar
                def _(x):
                    run("act", x)
            if per["dve"]:
                @block.vector
                def _(x):
                    run("dve", x)
            if per["pool"]:
                @block.gpsimd
                def _(x):
                    run("pool", x)
            if per["pe"]:
                @block.tensor
                def _(x):
                    run("pe", x)
        self.phase += 1


class Ring:
    def __init__(self, P, items, nslots, load):
        self.P = P
        self.items = items
        self.n = nslots
        self.load = load
        self.next = 0

    def get(self, i):
        while self.next < len(self.items) and self.next < i + self.n:
            self.load(self.items[self.next], self.next % self.n)
            self.next += 1
        return i % self.n


C_TL, C_TGE, C_UF, C_UB, C_MASK, C_PROT, C_FLAG, NCST = 0, 128, 256, 384, 512, 768, 800, 802
PP_QK, PP_CW, PP_CB, PP_DSK, PP_SN, PP_B1, PP_DWB, PP_LNG, PP_LNB, PP_DWW, PP_DTB, NPP = (
    0, 4, 100, 124, 140, 156, 220, 252, 284, 316, 1308, 1310)


def build(plan, debug=()):
    nc = bass.Bass("TRN2", target_bir_lowering=False)
    es = ExitStack()
    P = Prog(nc, es)
    T = plan.get("T", 4096)
    HALF = T // 2
    NCH = T // 128
    phases = plan["phases"]

    def din(name, shape, dt=F32):
        return nc.dram_tensor(name, list(shape), dt, kind="ExternalInput").ap()

    def dscr(name, shape, dt):
        kind = "ExternalOutput" if name in debug else "Internal"
        return nc.dram_tensor(name, list(shape), dt, kind=kind).ap()

    uid = [0]

    def sb(pes, name, shape, dt):
        uid[0] += 1
        return pes.enter_context(nc.sbuf_tensor("s%d_%s" % (uid[0], name), list(shape), dt))

    def pst(pes, name, shape, dt=F32):
        uid[0] += 1
        return pes.enter_context(nc.psum_tensor("p%d_%s" % (uid[0], name), list(shape), dt))

    x_in = din("x_in", [T, D])
    xres = nc.dram_tensor("y", [T, D], F32, kind="ExternalOutput").ap()
    wshapes = {
        "w_in": (2, D, IN_COLS), "w_out": (2, D, D), "pw1_w": (2, D, 2 * D), "pw2_w": (2, D, D),
        "w_gate": (DEPTH, D, FH), "w_up": (DEPTH, D, FH), "w_down": (DEPTH, FH, D),
    }
    W = {k: din(k, s) for k, s in wshapes.items() if any(kk == k for (kk, _) in plan["weights"])}
    wtile = {"w_in": (5632, 256), "pw1_w": (2 * D, 256), "w_gate": (FH, 256), "w_up": (FH, 256),
             "w_out": (D, 512), "pw2_w": (D, 512), "w_down": (D, 512)}
    Wt = {k: dscr(k + "_t", [wshapes[k][0], n // cb, 128, wshapes[k][1] // 128, cb], BF16)
          for k, (n, cb) in wtile.items()}
    wdt_b = dscr("w_in_dt", [2, D, 32], BF16)
    gains = din("gains", [8, D])
    pw2b_d = din("pw2_b", [2, D])
    cst_d = din("cst", [128, NCST])
    pp_d = din("pp", [128, NPP])
    alog_d = din("alog", [2, 32])
    rope_d = din("rope", [32, 2, T])

    qT_d = dscr("qT", [1024, T], BF16)
    kT_d = dscr("kT", [1024, T], BF16)
    vT_d = dscr("vT", [1024, T], BF16)
    zs_d = dscr("zsT", [1024, T], BF16)
    xbc_d = dscr("xbcT", [1536, T], BF16)
    del_d = dscr("delT", [32, T], F32)
    mix_d = dscr("mixT", [2048, T], BF16)
    u_d = dscr("uT", [2048, T], BF16)
    xc_d = dscr("xcT", [1536, T], BF16)
    xtok_d = dscr("xtok", [T, 1280], BF16)
    dtok_d = dscr("dtok", [T, 32], F32)
    diag_d = dscr("diagd", [16, 128, 31 * 128], BF16)

    wcv = {}
    order = []
    for l in range(DEPTH):
        if l % 2 == 0:
            order += [("w_in", l // 2), ("w_out", l // 2)]
        else:
            order += [("pw1_w", l // 2), ("pw2_w", l // 2)]
        order += [("w_gate", l), ("w_up", l), ("w_down", l)]
    with ExitStack() as pes:
        stf = [sb(pes, "stf0", [128, 16, 2048], F32)]
        stb = [sb(pes, "stb0", [128, 16 * 2048], BF16)]
        P.pool_ok = True
        ccnt = 0
        for (k, i) in order:
            if (k, i) not in plan["weights"]:
                continue
            b = P.buf("wcv", k, i)
            wcv[(k, i)] = b
            n, cb = wtile[k]
            kck = wshapes[k][1] // 128
            srcw = W[k][i].rearrange("(kc p) f -> p kc f", p=128)
            for c0 in range(0, n, 2048):
                cw = min(2048, n - c0)
                nb = cw // cb
                for k0 in range(0, kck, 16):
                    kg = min(16, kck - k0)
                    s_ = 0
                    bst, bsb = P.buf("stf", s_), P.buf("stb", s_)
                    h = kg // 2
                    P.dma("sp", lambda e, s_=s_, k0=k0, h=h, c0=c0, cw=cw, srcw=srcw: e.dma_start(
                        out=stf[s_][:, 0:h, 0:cw], in_=srcw[:, k0:k0 + h, c0:c0 + cw]), w=(bst,), key=bst)
                    P.dma("sp", lambda e, s_=s_, k0=k0, h=h, kg=kg, c0=c0, cw=cw, srcw=srcw: e.dma_start(
                        out=stf[s_][:, h:kg, 0:cw], in_=srcw[:, k0 + h:k0 + kg, c0:c0 + cw]), w=(bst,), key=bst)
                    for (eng, ka, kb) in (("dve", 0, h), ("act", h, kg)):
                        outv = stb[s_][:, 0:nb * kg * cb].rearrange("p (b k c) -> p k b c", b=nb, k=kg)[:, ka:kb]
                        inv = stf[s_][:, ka:kb, 0:cw].rearrange("p k (b c) -> p k b c", b=nb)
                        if eng == "dve":
                            P.op("dve", lambda e, outv=outv, inv=inv: e.tensor_copy(out=outv, in_=inv), r=(bst,), w=(bsb,))
                        else:
                            P.op("act", lambda e, outv=outv, inv=inv: e.copy(out=outv, in_=inv), r=(bst,), w=(bsb,))
                    dst = Wt[k][i, c0 // cb:c0 // cb + nb, :, k0:k0 + kg, :].rearrange("b p k c -> p b (k c)")
                    P.dma("act", lambda e, s_=s_, dst=dst, nb=nb, kg=kg, cb=cb: e.dma_start(
                        out=dst, in_=stb[s_][:, 0:nb * kg * cb].rearrange("p (b x) -> p b x", b=nb)),
                        r=(bsb,), w=(b,), key=bsb)
                    ccnt += 1
            if k == "w_in":
                bdt = P.buf("wdtcv", i)
                P.dma("pool", lambda e, i=i: e.dma_start(out=wdt_b[i], in_=W["w_in"][i, :, 5632:5664]), w=(b,), key=bdt)
        P.flush("cast")
    P.pool_ok = False

    cst = sb(es, "cst", [128, NCST], F32)
    pp = sb(es, "pp", [128, NPP], F32)
    ident_f = sb(es, "ident_f", [128, 128], F32)
    ident_b = sb(es, "ident_b", [128, 128], BF16)
    ones_f = sb(es, "ones_f", [128, 128], F32)
    ones_b = sb(es, "ones_b", [128, 128], BF16)
    eps_t = sb(es, "eps_t", [128, 1], F32)
    maskb = sb(es, "maskb", [128, 256], BF16)
    maskx = sb(es, "maskx", [128, 256], BF16)
    protb = sb(es, "protb", [128, 32], BF16)
    bC = P.buf("C")
    bcst = P.buf("cst")
    bpp = P.buf("pp")
    P.dma("sp", lambda e: e.dma_start(out=cst[:], in_=cst_d[:, :]), w=(bcst,), key=bcst)
    P.dma("sp", lambda e: e.dma_start(out=pp[:], in_=pp_d[:, :]), w=(bpp,), key=bpp)

    def mk_ident(e):
        e.memset(ones_f[:], 1.0)
        return e.affine_select(out=ident_f[:], in_=ones_f[:], pattern=[[-1, 128]],
                               compare_op=ALU.is_equal, fill=0.0, base=0, channel_multiplier=1)
    P.op("pool", mk_ident, w=(P.buf("c0"),))
    P.op("dve", lambda e: e.tensor_copy(out=ident_b[:], in_=ident_f[:]), r=(P.buf("c0"),), w=(P.buf("c1"),))
    P.op("dve", lambda e: e.tensor_copy(out=ones_b[:], in_=ones_f[:]), r=(P.buf("c0"),), w=(P.buf("c2"),))
    P.op("dve", lambda e: e.memset(eps_t[:], EPS), w=(P.buf("c3"),))
    P.op("dve", lambda e: e.tensor_copy(out=maskb[:], in_=cst[:, C_MASK:C_MASK + 256]), r=(bcst,), w=(P.buf("c4"),))
    P.op("dve", lambda e: e.tensor_scalar(out=maskx[:], in0=cst[:, C_MASK:C_MASK + 256],
                                          scalar1=cst[:, C_FLAG + 1:C_FLAG + 2], scalar2=None, op0=ALU.add),
         r=(bcst,), w=(P.buf("c5"),))
    P.op("dve", lambda e: e.tensor_copy(out=protb[:], in_=cst[:, C_PROT:C_PROT + 32]), r=(bcst,), w=(P.buf("c6"),))
    P.flush("const")
    keep_ap = cst[:, C_FLAG:C_FLAG + 1]
    TL = cst[:, C_TL:C_TL + 128]
    TGE = cst[:, C_TGE:C_TGE + 128]
    UF = cst[:, C_UF:C_UF + 128]
    UB = cst[:, C_UB:C_UB + 128]

    def load_bc(pes, name, src_row, n=D):
        t = sb(pes, name, [128, n], F32)
        b = P.buf(name)
        P.dma("sp", lambda e: e.dma_start(out=t[:], in_=src_row.partition_broadcast(128)), w=(b,), key=b)
        return t, b

    def norm_alloc(pes):
        xblk = [sb(pes, "xblk%d" % s, [128, D], F32) for s in range(2)]
        xs = [sb(pes, "xs%d" % s, [128, D], BF16) for s in range(2)]
        sq = None
        ss = [sb(pes, "ss%d" % s, [128, 2], F32) for s in range(2)]
        rstd = [sb(pes, "rstd%d" % s, [128, 1], F32) for s in range(2)]
        tp = [pst(pes, "tp%d" % s, [128, 1024], BF16) for s in range(2)]
        return (xblk, xs, sq, ss, rstd, tp)

    def norm_tile(t0, TT, src, gbc, gb, hT, hTb, bufs):
        xblk, xs, sq, ss, rstd, tp = bufs
        for tb in range(TT // 128):
            r0 = t0 + tb * 128
            s = tb % 2
            bx = P.buf("xblk", s)
            P.dma("sp", lambda e, s=s, r0=r0: e.dma_start(out=xblk[s][:], in_=src[r0:r0 + 128, :]),
                  r=(P.buf("xres", r0 // 128),), w=(bx,), key=bx)
            bss = P.buf("ss", s)
            bxs = P.buf("xs", s)
            P.op("act", lambda e, s=s: e.activation(out=xs[s][:], in_=xblk[s][:], func=AF.Square,
                                                     accum_out=ss[s][:, 0:1]),
                 r=(bx,), w=(bxs, bss))
            P.op("act", lambda e, s=s: e.activation(out=ss[s][:, 1:2], in_=ss[s][:, 0:1], func=AF.Sqrt,
                                                     scale=1.0 / D, bias=eps_t[:, 0:1]),
                 r=(bss,), w=(bss,))
            brs = P.buf("rstd", s)
            P.op("dve", lambda e, s=s: e.reciprocal(out=rstd[s][:], in_=ss[s][:, 1:2]), r=(bss,), w=(brs,))
            bxs = P.buf("xs", s)
            P.op("dve", lambda e, s=s: e.scalar_tensor_tensor(out=xs[s][:], in0=xblk[s][:], scalar=rstd[s][:, 0:1],
                                                               in1=gbc[:], op0=ALU.mult, op1=ALU.mult),
                 r=(bx, brs, gb), w=(bxs,))
            for hlf in range(2):
                btp = P.buf("tp", hlf)
                for c in range(8):
                    cc = hlf * 8 + c
                    P.op("pe", lambda e, s=s, cc=cc, c=c, hlf=hlf: e.transpose(
                        out=tp[hlf][:, c * 128:(c + 1) * 128], in_=xs[s][:, cc * 128:(cc + 1) * 128],
                        identity=ident_b[:]), r=(bxs,), w=(btp,))
                dst = hT[:, hlf * 8:(hlf + 1) * 8, tb * 128:(tb + 1) * 128]
                srcp = tp[hlf][:].rearrange("p (c t) -> p c t", c=8)
                if hlf == 0:
                    P.op("act", lambda e, dst=dst, srcp=srcp: e.copy(out=dst, in_=srcp), r=(btp,), w=(hTb,))
                else:
                    P.op("dve", lambda e, dst=dst, srcp=srcp: e.tensor_copy(out=dst, in_=srcp), r=(btp,), w=(hTb,))

    def g2_tile(t0, ntb, lhs_fn, lhs_bufs_fn, nk, wslot_fn, ring, ring_i, rsrc, ps, pcnt, xr, yo, rcnt, bias_bc=None):
        NR = len(xr)
        NP = len(ps)
        for db in range(4):
            s = ring.get(ring_i[0])
            ring_i[0] += 1
            wt, bwd = wslot_fn(s)
            for tb in range(ntb):
                r0 = t0 + tb * 128
                ri = rcnt[0] % NR
                rcnt[0] += 1
                bxr = P.buf("xr", ri)
                bxd = P.buf("xresd", r0 // 128, db)
                P.dma("sp", lambda e, ri=ri, r0=r0, db=db: e.dma_start(
                    out=xr[ri][:], in_=rsrc[r0:r0 + 128, db * 512:(db + 1) * 512]),
                    r=(bxd, P.buf("xres", r0 // 128)), w=(bxr,), key=bxr)
                if bias_bc is not None:
                    P.op("pool", lambda e, ri=ri, db=db: e.tensor_tensor(
                        out=xr[ri][:], in0=xr[ri][:], in1=bias_bc[0][:, db * 512:(db + 1) * 512], op=ALU.add),
                        r=(bxr, bias_bc[1]), w=(bxr,))
                po = ps[pcnt[0] % NP]
                bpo = P.buf("psg2", pcnt[0] % NP)
                pcnt[0] += 1
                for k in range(nk):
                    P.op("pe", lambda e, po=po, k=k, tb=tb, wt=wt: e.matmul(
                        po[:], lhs_fn(k, tb), wt[:, k, :], start=(k == 0), stop=(k == nk - 1)),
                        r=tuple(lhs_bufs_fn(k)) + (bwd,), w=(bpo,))
                byo = P.buf("yo", ri)
                P.op("dve", lambda e, po=po, ri=ri: e.tensor_tensor(out=yo[ri][:], in0=po[:], in1=xr[ri][:], op=ALU.add),
                     r=(bpo, bxr), w=(byo,))
                P.dma("act", lambda e, ri=ri, r0=r0, db=db: e.dma_start(
                    out=xres[r0:r0 + 128, db * 512:(db + 1) * 512], in_=yo[ri][:]),
                    r=(byo,), w=(bxd,), key=byo)

    def collapse_rows(t0, ntb):
        for tb in range(ntb):
            r0 = t0 + tb * 128
            P.op("pool", lambda e: e.nop(), r=tuple(P.buf("xresd", r0 // 128, db) for db in range(4)),
                 w=(P.buf("xres", r0 // 128),))

    def g1_tile(TT, hT, hTb, wt, ring, ring_i, nblk, M, ps, pcnt, epilogue, ncols_last=None):
        NP = len(ps)
        for blk in range(nblk):
            s = ring.get(ring_i[0])
            ring_i[0] += 1
            bw = [P.buf("wg1", s, m) for m in range(M)]
            nfc = 2
            for fc in range(nfc):
                f = blk * 2 + fc
                for half in range(TT // 512):
                    pl = []
                    for m in range(M):
                        pl.append((ps[pcnt[0] % NP], P.buf("psg1", pcnt[0] % NP)))
                        pcnt[0] += 1
                    for k in range(KC):
                        for m in range(M):
                            pp_, bp = pl[m]
                            P.op("pe", lambda e, pp_=pp_, s=s, m=m, k=k, fc=fc, half=half: e.matmul(
                                pp_[:], wt[s][m][:, k, fc * 128:(fc + 1) * 128],
                                hT[:, k, half * 512:(half + 1) * 512], start=(k == 0), stop=(k == KC - 1)),
                                r=(bw[m], hTb), w=(bp,))
                    epilogue(f, half, [p for p, _ in pl], [b for _, b in pl])

    def g1_loader(wt, srcs, cvs, bases):
        def load(item, s):
            blk = item[-1]
            for m in range(len(srcs)):
                b = P.buf("wg1", s, m)
                bi = bases[m] + blk
                P.dma("sp", lambda e, s=s, m=m, bi=bi: e.dma_start(out=wt[s][m][:], in_=srcs[m][bi]),
                      r=(cvs[m],), w=(b,), key=b)
        return load

    def ffn_layer(l, first_src):
        TT = 1024
        NT = T // TT
        HB = FC // 2
        with ExitStack() as pes:
            P.pool_ok = True
            gbc, gb = load_bc(pes, "gbc", gains[4 + l:5 + l, :])
            hT = sb(pes, "hT", [128, KC, TT], BF16)
            aT = sb(pes, "aT", [128, HB, TT], BF16)
            wt = [[sb(pes, "wgu%d_%d" % (s, m), [128, KC, 256], BF16) for m in range(2)] for s in range(2)]
            wd = [sb(pes, "wd%d" % s, [128, HB, 512], BF16) for s in range(2)]
            nb = norm_alloc(pes)
            xr = [sb(pes, "xr%d" % s, [128, 512], F32) for s in range(2)]
            yo = [sb(pes, "yo%d" % s, [128, 512], F32) for s in range(2)]
            sg = [sb(pes, "sg%d" % s, [128, 512], F32) for s in range(2)]
            ps = [pst(pes, "ps%d" % s, [128, 512]) for s in range(6)]
            hTb = P.buf("hT")
            wg_d = Wt["w_gate"][l]
            wu_d = Wt["w_up"][l]
            wd_d = Wt["w_down"][l]
            cvg, cvu, cvd = wcv[("w_gate", l)], wcv[("w_up", l)], wcv[("w_down", l)]
            nblk = HB // 2
            gu_items = [(tt, hf, hf * nblk + blk) for tt in range(NT) for hf in range(2) for blk in range(nblk)]
            d_items = [(tt, hf, db) for tt in range(NT) for hf in range(2) for db in range(4)]

            def load_d(item, s):
                tt, hf, db = item
                b = P.buf("wd", s)
                P.dma("sp", lambda e, s=s, hf=hf, db=db: e.dma_start(
                    out=wd[s][:], in_=wd_d[db, :, hf * HB:(hf + 1) * HB, :]), r=(cvd,), w=(b,), key=b)

            ring_gu = Ring(P, gu_items, 2, g1_loader(wt, [wg_d, wu_d], [cvg, cvu], [0, 0]))
            ring_d = Ring(P, d_items, 2, load_d)
            gi, di, pcnt, rcnt, sgc = [0], [0], [0], [0], [0]
            for tt in range(NT):
                src = first_src if first_src is not None else xres
                norm_tile(tt * TT, TT, src, gbc, gb, hT, hTb, nb)
                for hf in range(2):
                    def epi(f, half, pl, bl):
                        sgi = sgc[0] % 2
                        sgc[0] += 1
                        bsg = P.buf("sg", sgi)
                        P.op("act", lambda e: e.activation(out=sg[sgi][:], in_=pl[0][:], func=AF.Silu),
                             r=(bl[0],), w=(bsg,))
                        P.op("dve", lambda e: e.tensor_tensor(
                            out=aT[:, f, half * 512:(half + 1) * 512], in0=sg[sgi][:], in1=pl[1][:], op=ALU.mult),
                            r=(bsg, bl[1]), w=(P.buf("aT", f),))
                    g1_tile(TT, hT, hTb, wt, ring_gu, gi, nblk, 2, ps, pcnt, epi)
                    rsrc = src if hf == 0 else xres
                    g2_tile(tt * TT, TT // 128, lambda k, tb: aT[:, k, tb * 128:(tb + 1) * 128],
                            lambda k: (P.buf("aT", k),), HB, lambda s: (wd[s], P.buf("wd", s)),
                            ring_d, di, rsrc, ps, pcnt, xr, yo, rcnt)
                collapse_rows(tt * TT, TT // 128)
            P.flush("ffn%d" % l)
            P.pool_ok = False

    def odd_layer(o, first_src):
        TT = 1024
        NT = T // TT
        cvw1, cvw2 = wcv[("pw1_w", o)], wcv[("pw2_w", o)]
        w1_d = Wt["pw1_w"][o]
        w2_d = Wt["pw2_w"][o]
        src = first_src if first_src is not None else xres
        with ExitStack() as pes:
            P.pool_ok = True
            gbc, gb = load_bc(pes, "gbc", gains[2 + o:3 + o, :])
            hT = sb(pes, "hT", [128, KC, TT], BF16)
            wt = [[sb(pes, "w1_%d_%d" % (s, m), [128, KC, 256], BF16) for m in range(2)] for s in range(2)]
            nb = norm_alloc(pes)
            sg = [sb(pes, "sg%d" % s, [128, 512], F32) for s in range(2)]
            ust = [sb(pes, "ust%d" % s, [128, 1024], BF16) for s in range(3)]
            ps = [pst(pes, "ps%d" % s, [128, 512]) for s in range(6)]
            hTb = P.buf("hT")
            items = [(tt, blk) for tt in range(NT) for blk in range(8)]
            ring = Ring(P, items, 2, g1_loader(wt, [w1_d, w1_d], [cvw1, cvw1], [0, D // 256]))
            gi, pcnt, cnt = [0], [0], [0]
            for tt in range(NT):
                norm_tile(tt * TT, TT, src, gbc, gb, hT, hTb, nb)

                def epi(f, half, pl, bl, tt=tt):
                    i = cnt[0]
                    cnt[0] += 1
                    sgi, ui = i % 2, (i // 2) % 3
                    bsg, bu = P.buf("sg", sgi), P.buf("ust", ui)
                    P.op("act", lambda e: e.activation(out=sg[sgi][:], in_=pl[1][:], func=AF.Sigmoid,
                                                       bias=pp[:, PP_B1 + o * 32 + 16 + f:PP_B1 + o * 32 + 17 + f]),
                         r=(bl[1], bpp), w=(bsg,))
                    P.op("dve", lambda e: e.scalar_tensor_tensor(
                        out=ust[ui][:, half * 512:(half + 1) * 512], in0=pl[0][:], scalar=pp[:, PP_B1 + o * 32 + f:PP_B1 + o * 32 + f + 1],
                        in1=sg[sgi][:], op0=ALU.add, op1=ALU.mult), r=(bl[0], bsg, bpp), w=(bu,))
                    c0 = tt * TT
                    if half == 1:
                        P.dma("act", lambda e: e.dma_start(out=u_d[f * 128:(f + 1) * 128, c0:c0 + TT], in_=ust[ui][:]),
                              r=(bu,), w=(P.buf("uT", c0 // 512), P.buf("uT", c0 // 512 + 1)), key=bu)
                g1_tile(TT, hT, hTb, wt, ring, gi, 8, 2, ps, pcnt, epi)
            P.flush("odd%d_1" % o)
            P.pool_ok = False

        with ExitStack() as pes:
            dg = [sb(pes, "dg%d" % s, [128, 31 * 128], BF16) for s in range(3)]
            uh = [sb(pes, "uh%d" % s, [128, 16, 542], BF16) for s in range(2)]
            uf = sb(pes, "uf", [128, 16, 512], F32)
            usq = [sb(pes, "usq%d" % s, [128, 512], F32) for s in range(2)]
            vT = sb(pes, "vTt", [128, 16, 512], BF16)
            w2 = [sb(pes, "w2_%d" % s, [128, KC, 512], BF16) for s in range(2)]
            pbc, pbb = load_bc(pes, "pbc", pw2b_d[o:o + 1, :])
            mean = sb(pes, "mean", [128, 512], F32)
            m2 = sb(pes, "m2", [128, 512], F32)
            rs = sb(pes, "rs", [128, 512], F32)
            t1 = [sb(pes, "t1_%d" % s, [128, 512], F32) for s in range(2)]
            xr = [sb(pes, "xr%d" % s, [128, 512], F32) for s in range(3)]
            yo = [sb(pes, "yo%d" % s, [128, 512], F32) for s in range(3)]
            psc = [pst(pes, "psc%d" % s, [128, 512]) for s in range(2)]
            pss = [pst(pes, "pss%d" % s, [128, 512]) for s in range(2)]
            pso = [pst(pes, "pso%d" % s, [128, 512]) for s in range(3)]
            for c in range(16):
                s = c % 3
                bd = P.buf("dg", s)
                dw = pp[:, PP_DWW + (o * 16 + c) * 31:PP_DWW + (o * 16 + c + 1) * 31]
                P.op("dve", lambda e, s=s, dw=dw: e.tensor_tensor(
                    out=dg[s][:].rearrange("p (j i) -> p j i", j=31),
                    in0=ident_f[:].unsqueeze(1).to_broadcast([128, 31, 128]),
                    in1=dw.unsqueeze(2).to_broadcast([128, 31, 128]), op=ALU.mult), r=(bpp,), w=(bd,))
                P.dma("act", lambda e, s=s, c=c: e.dma_start(out=diag_d[c], in_=dg[s][:]), r=(bd,),
                      w=(P.buf("diagd", c),), key=bd)
            NTL = T // 512
            dg_items = [(tl, c) for tl in range(NTL) for c in range(16)]

            def load_dg(item, s):
                tl, c = item
                b = P.buf("dg", s)
                P.dma("sp", lambda e: e.dma_start(out=dg[s][:], in_=diag_d[c]), r=(P.buf("diagd", c),), w=(b,), key=b)
            ring_dg = Ring(P, dg_items, 3, load_dg)
            w_items = [(tl, db) for tl in range(NTL) for db in range(4)]

            def load_w2(item, s):
                tl, db = item
                b = P.buf("w2", s)
                P.dma("sp", lambda e: e.dma_start(out=w2[s][:], in_=w2_d[db]), r=(cvw2,), w=(b,), key=b)
            ring_w2 = Ring(P, w_items, 2, load_w2)
            dgi, wi, pcnt, rcnt = 0, [0], [0], [0]
            bvT = P.buf("vTt")
            for tl in range(NTL):
                t0 = tl * 512
                us = tl % 2
                buh = P.buf("uh", us)
                lo = max(t0 - 15, 0)
                hi = min(t0 + 527, T)
                urows = u_d.rearrange("(c p) t -> p c t", p=128)
                rdeps = tuple(P.buf("uT", j) for j in range(lo // 512, (hi - 1) // 512 + 1))
                if t0 == 0:
                    P.op("pool", lambda e, us=us: e.memset(uh[us][:, :, 0:15], 0.0), w=(buh,))
                if t0 + 512 == T:
                    P.op("pool", lambda e, us=us: e.memset(uh[us][:, :, 527:542], 0.0), w=(buh,))
                for q in range(2):
                    P.dma("sp", lambda e, us=us, q=q, lo=lo, hi=hi, t0=t0: e.dma_start(
                        out=uh[us][:, q * 8:(q + 1) * 8, 15 + lo - t0:15 + hi - t0],
                        in_=urows[:, q * 8:(q + 1) * 8, lo:hi]), r=rdeps, w=(buh,), key=buh)
                if t0 == HALF:
                    P.op("pool", lambda e, us=us: e.tensor_scalar(out=uh[us][:, :, 0:15], in0=uh[us][:, :, 0:15],
                                                                 scalar1=keep_ap, scalar2=None, op0=ALU.mult),
                         r=(buh,), w=(buh,))
                if t0 + 512 == HALF:
                    P.op("pool", lambda e, us=us: e.tensor_scalar(out=uh[us][:, :, 527:542], in0=uh[us][:, :, 527:542],
                                                                 scalar1=keep_ap, scalar2=None, op0=ALU.mult),
                         r=(buh,), w=(buh,))
                bsum, bsq = P.buf("pss", 0), P.buf("pss", 1)
                for c in range(16):
                    s = ring_dg.get(dgi)
                    dgi += 1
                    bd = P.buf("dg", s)
                    pc = psc[c % 2]
                    bpc = P.buf("psc", c % 2)
                    for j in range(31):
                        P.op("pe", lambda e, pc=pc, s=s, j=j, c=c, us=us: e.matmul(
                            pc[:], dg[s][:, j * 128:(j + 1) * 128], uh[us][:, c, j:j + 512],
                            start=(j == 0), stop=(j == 30)), r=(bd, buh), w=(bpc,))
                    buf_ = P.buf("uf", c)
                    P.op("act", lambda e, pc=pc, c=c: e.activation(
                        out=uf[:, c, :], in_=pc[:], func=AF.Identity,
                        bias=pp[:, PP_DWB + o * 16 + c:PP_DWB + o * 16 + c + 1]), r=(bpc, bpp), w=(buf_,))
                    busq = P.buf("usq", c % 2)
                    P.op("act", lambda e, c=c: e.activation(out=usq[c % 2][:], in_=uf[:, c, :], func=AF.Square),
                         r=(buf_,), w=(busq,))
                    P.op("pe", lambda e, c=c: e.matmul(pss[0][:], ones_f[:], uf[:, c, :], start=(c == 0), stop=(c == 15)),
                         r=(buf_,), w=(bsum,))
                    P.op("pe", lambda e, c=c: e.matmul(pss[1][:], ones_f[:], usq[c % 2][:], start=(c == 0), stop=(c == 15)),
                         r=(busq,), w=(bsq,))
                bst = P.buf("lnstat")
                P.op("act", lambda e: e.activation(out=mean[:], in_=pss[0][:], func=AF.Copy, scale=1.0 / D),
                     r=(bsum,), w=(P.buf("mean"),))
                P.op("dve", lambda e: e.tensor_tensor(out=m2[:], in0=mean[:], in1=mean[:], op=ALU.mult),
                     r=(P.buf("mean"),), w=(P.buf("m2"),))
                P.op("dve", lambda e: e.scalar_tensor_tensor(out=m2[:], in0=pss[1][:], scalar=1.0 / D, in1=m2[:],
                                                             op0=ALU.mult, op1=ALU.subtract),
                     r=(bsq, P.buf("m2")), w=(P.buf("m2"),))
                P.op("act", lambda e: e.activation(out=m2[:], in_=m2[:], func=AF.Sqrt, bias=eps_t[:, 0:1]),
                     r=(P.buf("m2"),), w=(P.buf("m2"),))
                P.op("dve", lambda e: e.reciprocal(out=rs[:], in_=m2[:]), r=(P.buf("m2"),), w=(bst,))
                for c in range(16):
                    ti = c % 2
                    bt = P.buf("t1", ti)
                    P.op("dve", lambda e, c=c, ti=ti: e.tensor_tensor(out=t1[ti][:], in0=uf[:, c, :], in1=mean[:], op=ALU.subtract),
                         r=(P.buf("uf", c), P.buf("mean")), w=(bt,))
                    P.op("pool", lambda e, ti=ti: e.tensor_tensor(out=t1[ti][:], in0=t1[ti][:], in1=rs[:], op=ALU.mult),
                         r=(bt, bst), w=(bt,))
                    P.op("act", lambda e, c=c, ti=ti: e.activation(
                        out=vT[:, c, :], in_=t1[ti][:], func=AF.Silu,
                        scale=pp[:, PP_LNG + o * 16 + c:PP_LNG + o * 16 + c + 1],
                        bias=pp[:, PP_LNB + o * 16 + c:PP_LNB + o * 16 + c + 1]), r=(bt, bpp), w=(bvT,))
                g2_tile(t0, 4, lambda k, tb: vT[:, k, tb * 128:(tb + 1) * 128], lambda k: (bvT,), KC,
                        lambda s: (w2[s], P.buf("w2", s)), ring_w2, wi, src, pso, pcnt, xr, yo, rcnt, bias_bc=(pbc, pbb))
                collapse_rows(t0, 4)
            P.flush("odd%d_2" % o)

    def even_layer(e_, first_src):
        src = first_src if first_src is not None else xres
        cvi, cvo = wcv[("w_in", e_)], wcv[("w_out", e_)]
        wi_d = Wt["w_in"][e_]
        wo_d = Wt["w_out"][e_]
        wdt_d = wdt_b[e_].rearrange("(kc p) c -> p kc c", p=128)
        SCALE = 128.0 ** -0.5
        if ("e1", e_) in phases:
            TT = 1024
            NT = T // TT
            with ExitStack() as pes:
                P.pool_ok = True
                gbc, gb = load_bc(pes, "gbc", gains[e_:e_ + 1, :])
                hT = sb(pes, "hT", [128, KC, TT], BF16)
                wt = [[sb(pes, "wi%d" % s, [128, KC, 256], BF16)] for s in range(3)]
                wdt = sb(pes, "wdt", [128, KC, 32], BF16)
                rope = sb(pes, "rope", [32, 2, T], F32)
                nb = norm_alloc(pes)
                sqb = [sb(pes, "sqb%d" % s, [128, 512], BF16) for s in range(2)]
                sd = [sb(pes, "sd%d" % s, [128, 512], F32) for s in range(2)]
                qnb = [sb(pes, "qnb%d" % s, [128, 1024], BF16) for s in range(4)]
                r1 = [sb(pes, "r1_%d" % s, [32, 512], F32) for s in range(2)]
                r2 = [sb(pes, "r2_%d" % s, [32, 512], F32) for s in range(2)]
                dl = [sb(pes, "dl%d" % s, [32, 512], F32) for s in range(2)]
                ps = [pst(pes, "ps%d" % s, [128, 512]) for s in range(3)]
                pss = [pst(pes, "pss%d" % s, [128, 512]) for s in range(2)]
                psr = pst(pes, "psr", [128, 512])
                hTb = P.buf("hT")
                brope, bwdt = P.buf("rope"), P.buf("wdt")
                P.dma("sp", lambda e: e.dma_start(out=rope[:], in_=rope_d[:, :, :]), w=(brope,), key=brope)
                P.dma("sp", lambda e: e.dma_start(out=wdt[:], in_=wdt_d), r=(cvi,), w=(bwdt,), key=bwdt)
                items = [(tt, blk) for tt in range(NT) for blk in range(22)]
                ring = Ring(P, items, 3, g1_loader(wt, [wi_d], [cvi], [0]))
                gi, pcnt, cnt = [0], [0], [0]
                for tt in range(NT):
                    norm_tile(tt * TT, TT, src, gbc, gb, hT, hTb, nb)

                    def epi(f, half, pl, bl, tt=tt):
                        i = cnt[0]
                        cnt[0] += 1
                        c0 = tt * TT + half * 512
                        p, bp = pl[0], bl[0]
                        qi = (i // 2) % 4
                        qv = qnb[qi][:, half * 512:(half + 1) * 512]
                        qv32 = qnb[qi][0:32, half * 512:(half + 1) * 512]
                        bq = P.buf("qnb", qi)
                        if f < 16:
                            si = i % 2
                            bsq, bsd, bps = P.buf("sqb", si), P.buf("sd", si), P.buf("pss", si)
                            P.op("act", lambda e: e.activation(out=sqb[si][:], in_=p[:], func=AF.Square), r=(bp,), w=(bsq,))
                            P.op("pe", lambda e: e.matmul(pss[si][:], ones_b[:], sqb[si][:], start=True, stop=True),
                                 r=(bsq,), w=(bps,))
                            P.op("act", lambda e: e.activation(out=sd[si][:], in_=pss[si][:], func=AF.Sqrt,
                                                               scale=1.0 / 128, bias=eps_t[:, 0:1]), r=(bps,), w=(bsd,))
                            P.op("dve", lambda e: e.reciprocal(out=sd[si][:], in_=sd[si][:]), r=(bsd,), w=(bsd,))
                            gcol = PP_QK + e_ * 2 + (1 if f >= 8 else 0)
                            P.op("dve", lambda e: e.scalar_tensor_tensor(
                                out=qv, in0=p[:], scalar=pp[:, gcol:gcol + 1], in1=sd[si][:],
                                op0=ALU.mult, op1=ALU.mult), r=(bp, bsd, bpp), w=(bq,))
                            bpr = P.buf("psr")
                            P.op("pe", lambda e: e.matmul(psr[0:32, :], protb[:], qv, start=True, stop=True),
                                 r=(bq,), w=(bpr,))
                            br1, br2 = P.buf("r1", si), P.buf("r2", si)
                            P.op("dve", lambda e: e.tensor_tensor(out=r1[si][:], in0=qv32,
                                                                  in1=rope[:, 0, c0:c0 + 512], op=ALU.mult),
                                 r=(bq, brope), w=(br1,))
                            P.op("dve", lambda e: e.tensor_tensor(out=r2[si][:], in0=psr[0:32, :],
                                                                  in1=rope[:, 1, c0:c0 + 512], op=ALU.mult),
                                 r=(bpr, brope), w=(br2,))
                            P.op("dve", lambda e: e.tensor_tensor(out=qv32, in0=r1[si][:], in1=r2[si][:],
                                                                  op=ALU.add), r=(br1, br2), w=(bq,))
                            dst = (qT_d if f < 8 else kT_d)[(f % 8) * 128:(f % 8 + 1) * 128, tt * TT:(tt + 1) * TT]
                        elif f < 24:
                            P.op("act", lambda e: e.copy(out=qv, in_=p[:]), r=(bp,), w=(bq,))
                            dst = vT_d[(f - 16) * 128:(f - 15) * 128, tt * TT:(tt + 1) * TT]
                        elif f < 32:
                            P.op("act", lambda e: e.activation(out=qv, in_=p[:], func=AF.Silu), r=(bp,), w=(bq,))
                            dst = zs_d[(f - 24) * 128:(f - 23) * 128, tt * TT:(tt + 1) * TT]
                        else:
                            P.op("dve", lambda e: e.tensor_copy(out=qv, in_=p[:]), r=(bp,), w=(bq,))
                            dst = xbc_d[(f - 32) * 128:(f - 31) * 128, tt * TT:(tt + 1) * TT]
                        if half == 1:
                            P.dma("act", lambda e: e.dma_start(out=dst, in_=qnb[qi][:]), r=(bq,), w=(P.buf("e1o", i),), key=bq)
                    g1_tile(TT, hT, hTb, wt, ring, gi, 22, 1, ps, pcnt, epi)
                    for half in range(2):
                        c0 = tt * TT + half * 512
                        p = ps[pcnt[0] % 3]
                        bp = P.buf("psg1", pcnt[0] % 3)
                        pcnt[0] += 1
                        for k in range(KC):
                            P.op("pe", lambda e, p=p, k=k, half=half: e.matmul(
                                p[0:32, :], wdt[:, k, :], hT[:, k, half * 512:(half + 1) * 512],
                                start=(k == 0), stop=(k == KC - 1)), r=(bwdt, hTb), w=(bp,))
                        di = half
                        bd = P.buf("dl", di)
                        P.op("act", lambda e, p=p, di=di: e.activation(
                            out=dl[di][:], in_=p[0:32, :], func=AF.Exp, bias=pp[0:32, PP_DTB + e_:PP_DTB + e_ + 1]),
                            r=(bp, bpp), w=(bd,))
                        P.op("act", lambda e, di=di: e.activation(out=dl[di][:], in_=dl[di][:], func=AF.Ln, bias=ones_f[0:32, 0:1]),
                             r=(bd,), w=(bd,))
                        P.dma("act", lambda e, di=di, c0=c0: e.dma_start(out=del_d[:, c0:c0 + 512], in_=dl[di][:]),
                              r=(bd,), w=(P.buf("e1d", c0),), key=bd)
                P.flush("e1_%d" % e_)
                P.pool_ok = False

        if ("e2", e_) in phases:
            with ExitStack() as pes:
                qs = [sb(pes, "qs%d" % s, [128, T], BF16) for s in range(2)]
                ks = [sb(pes, "ks%d" % s, [128, T], BF16) for s in range(2)]
                vs = [sb(pes, "vs%d" % s, [128, T], BF16) for s in range(2)]
                accn = sb(pes, "accn", [128, T], F32)
                accd = sb(pes, "accd", [128, T], F32)
                vtk = [sb(pes, "vtk%d" % s, [128, 32, 128], BF16) for s in range(2)]
                pT = [sb(pes, "pT%d" % s, [128, 256], BF16) for s in range(3)]
                att = [sb(pes, "att%d" % s, [128, T], BF16) for s in range(2)]
                psO = [pst(pes, "psO%d" % s, [128, 512]) for s in range(2)]
                psD = [pst(pes, "psD%d" % s, [128, 512]) for s in range(2)]
                psS = [pst(pes, "psS%d" % s, [128, 512]) for s in range(2)]
                ptv = pst(pes, "ptv", [128, 1024], BF16)
                pcs, sbc, vbc = 0, 0, 0
                baccn, baccd = P.buf("accn"), P.buf("accd")
                for h in range(8):
                    hs = h % 2
                    bq, bk, bv = P.buf("qs", hs), P.buf("ks", hs), P.buf("vs", hs)
                    for (tl, bb, srcd) in ((qs, bq, qT_d), (ks, bk, kT_d), (vs, bv, vT_d)):
                        for q in range(2):
                            P.dma("sp", lambda e, tl=tl, srcd=srcd, q=q, hs=hs, h=h: e.dma_start(
                                out=tl[hs][:, q * HALF:(q + 1) * HALF], in_=srcd[h * 128:(h + 1) * 128, q * HALF:(q + 1) * HALF]),
                                w=(bb,), key=bb)
                    for bi, dil in enumerate((1, 4, 16)):
                        L = T // dil
                        NKT = L // 128
                        SB = min(512, L)
                        mid = L // 2
                        vt = vtk[vbc % 2]
                        bvt = P.buf("vtk", vbc % 2)
                        vbc += 1
                        bptv = P.buf("ptv")
                        for grp in range(4):
                            for i8 in range(8):
                                idx = grp * 8 + i8
                                res, j = idx // NKT, idx % NKT
                                a0 = res + dil * 128 * j
                                P.op("pe", lambda e, i8=i8, a0=a0, dil=dil, hs=hs: e.transpose(
                                    out=ptv[:, i8 * 128:(i8 + 1) * 128], in_=vs[hs][:, a0:a0 + dil * 127 + 1:dil],
                                    identity=ident_b[:]), r=(bv,), w=(bptv,))
                            P.op("dve", lambda e, vt=vt, grp=grp: e.tensor_copy(
                                out=vt[:, grp * 8:(grp + 1) * 8, :], in_=ptv[:].rearrange("p (a b) -> p a b", a=8)),
                                r=(bptv,), w=(bvt,))
                        for res in range(dil):
                            for s in range(L // SB):
                                Q0, Q1 = SB * s, SB * (s + 1)
                                pieces = []
                                for j in range(max(Q0 // 128 - 1, 0), min(Q1 // 128 + 1, NKT)):
                                    a, b = max(128 * j - 64, Q0), min(128 * j + 192, Q1)
                                    for (pa, pb) in ((a, min(b, mid)), (max(a, mid), b)):
                                        if pa < pb:
                                            same = (pa >= mid) == (128 * j >= mid)
                                            pieces.append((j, pa, pb, same))
                                po, pd = psO[sbc % 2], psD[sbc % 2]
                                bpo, bpd = P.buf("psO", sbc % 2), P.buf("psD", sbc % 2)
                                sbc += 1
                                for pi, (j, pa, pb, same) in enumerate(pieces):
                                    w_ = pb - pa
                                    c0 = pa - (128 * j - 64)
                                    pS = psS[pcs % 2]
                                    bpS = P.buf("psS", pcs % 2)
                                    pt = pT[pcs % 3]
                                    bpt = P.buf("pT", pcs % 3)
                                    pcs += 1
                                    k0 = res + dil * 128 * j
                                    q0 = res + dil * pa
                                    mk = maskb if same else maskx
                                    P.op("pe", lambda e, pS=pS, k0=k0, q0=q0, w_=w_, dil=dil, hs=hs: e.matmul(
                                        pS[:, 0:w_], ks[hs][:, k0:k0 + dil * 127 + 1:dil],
                                        qs[hs][:, q0:q0 + dil * (w_ - 1) + 1:dil], start=True, stop=False),
                                        r=(bk, bq), w=(bpS,))
                                    P.op("pe", lambda e, pS=pS, mk=mk, c0=c0, w_=w_: e.matmul(
                                        pS[:, 0:w_], ident_b[:], mk[:, c0:c0 + w_], start=False, stop=True),
                                        r=(), w=(bpS,))
                                    P.op("act", lambda e, pS=pS, pt=pt, w_=w_: e.activation(
                                        out=pt[:, 0:w_], in_=pS[:, 0:w_], func=AF.Exp, scale=SCALE), r=(bpS,), w=(bpt,))
                                    first, last = (pi == 0), (pi == len(pieces) - 1)
                                    o0 = pa - Q0
                                    vi = res * NKT + j
                                    P.op("pe", lambda e, po=po, vt=vt, vi=vi, pt=pt, o0=o0, w_=w_, first=first, last=last: e.matmul(
                                        po[:, o0:o0 + w_], vt[:, vi, :], pt[:, 0:w_], start=first, stop=last,
                                        skip_group_check=True), r=(bvt, bpt), w=(bpo,))
                                    P.op("pe", lambda e, pd=pd, pt=pt, o0=o0, w_=w_, first=first, last=last: e.matmul(
                                        pd[:, o0:o0 + w_], ones_b[:], pt[:, 0:w_], start=first, stop=last,
                                        skip_group_check=True), r=(bpt,), w=(bpd,))
                                t0 = res + dil * Q0
                                sl = slice(t0, t0 + dil * (SB - 1) + 1, dil)
                                if bi == 0:
                                    P.op("act", lambda e, po=po, sl=sl, SB=SB: e.copy(out=accn[:, sl], in_=po[:, 0:SB]),
                                         r=(bpo,), w=(baccn,))
                                    P.op("dve", lambda e, pd=pd, sl=sl, SB=SB: e.tensor_copy(out=accd[:, sl], in_=pd[:, 0:SB]),
                                         r=(bpd,), w=(baccd,))
                                else:
                                    P.op("dve", lambda e, po=po, sl=sl, SB=SB: e.tensor_tensor(
                                        out=accn[:, sl], in0=accn[:, sl], in1=po[:, 0:SB], op=ALU.add), r=(bpo, baccn), w=(baccn,))
                                    P.op("dve", lambda e, pd=pd, sl=sl, SB=SB: e.tensor_tensor(
                                        out=accd[:, sl], in0=accd[:, sl], in1=pd[:, 0:SB], op=ALU.add), r=(bpd, baccd), w=(baccd,))
                    bat = P.buf("att", hs)
                    for q in range(4):
                        qsl = slice(q * (T // 4), (q + 1) * (T // 4))
                        P.op("dve", lambda e, qsl=qsl: e.reciprocal(out=accd[:, qsl], in_=accd[:, qsl]), r=(baccd,), w=(baccd,))
                        P.op("pool", lambda e, qsl=qsl, hs=hs: e.tensor_tensor(out=att[hs][:, qsl], in0=accn[:, qsl], in1=accd[:, qsl],
                                                                         op=ALU.mult), r=(baccn, baccd), w=(bat,))
                    for q in range(2):
                        P.dma("act", lambda e, q=q, h=h, hs=hs: e.dma_start(
                            out=mix_d[h * 128:(h + 1) * 128, q * HALF:(q + 1) * HALF], in_=att[hs][:, q * HALF:(q + 1) * HALF]),
                            r=(bat,), w=(P.buf("mixo", h, q),), key=bat)
                P.flush("e2_%d" % e_)

        if ("e3", e_) in phases:
            ssd_layer(e_)

        if ("e4", e_) in phases:
            with ExitStack() as pes:
                P.pool_ok = True
                mx = [sb(pes, "mx%d" % s, [128, KC, 512], BF16) for s in range(2)]
                wo = [sb(pes, "wo%d" % s, [128, KC, 512], BF16) for s in range(2)]
                xr = [sb(pes, "xr%d" % s, [128, 512], F32) for s in range(3)]
                yo = [sb(pes, "yo%d" % s, [128, 512], F32) for s in range(3)]
                pso = [pst(pes, "pso%d" % s, [128, 512]) for s in range(4)]
                NTL = T // 512
                w_items = [(tl, db) for tl in range(NTL) for db in range(4)]

                def load_wo(item, s):
                    tl, db = item
                    b = P.buf("wo", s)
                    P.dma("sp", lambda e: e.dma_start(out=wo[s][:], in_=wo_d[db]), r=(cvo,), w=(b,), key=b)
                ring_wo = Ring(P, w_items, 2, load_wo)
                wi, pcnt, rcnt = [0], [0], [0]
                mrows = mix_d.rearrange("(c p) t -> p c t", p=128)
                for tl in range(NTL):
                    t0 = tl * 512
                    ms = tl % 2
                    bm = P.buf("mx", ms)
                    for q in range(2):
                        P.dma("sp", lambda e, q=q, ms=ms, t0=t0: e.dma_start(
                            out=mx[ms][:, q * 8:(q + 1) * 8, :], in_=mrows[:, q * 8:(q + 1) * 8, t0:t0 + 512]),
                            w=(bm,), key=bm)
                    g2_tile(t0, 4, lambda k, tb, ms=ms: mx[ms][:, k, tb * 128:(tb + 1) * 128], lambda k, bm=bm: (bm,), KC,
                            lambda s: (wo[s], P.buf("wo", s)), ring_wo, wi, src, pso, pcnt, xr, yo, rcnt)
                    collapse_rows(t0, 4)
                P.flush("e4_%d" % e_)
                P.pool_ok = False

    def ssd_layer(e_):
        xbr = xbc_d.rearrange("(c p) t -> p c t", p=128)
        xcr = xc_d.rearrange("(c p) t -> p c t", p=128)
        zsr = zs_d.rearrange("(c p) t -> p c t", p=128)
        mxr = mix_d.rearrange("(c p) t -> p c t", p=128)
        with ExitStack() as pes:
            cdg = sb(pes, "cdg", [128, 12, 4, 128], BF16)
            xh = [sb(pes, "xh%d" % s, [128, 12, 515], BF16) for s in range(2)]
            xcb = sb(pes, "xcb", [128, 12, 512], BF16)
            xtk = [sb(pes, "xtk%d" % s, [128, 1280], BF16) for s in range(2)]
            dlt = [sb(pes, "dlt%d" % s, [32, 512], F32) for s in range(2)]
            dtk = [sb(pes, "dtk%d" % s, [128, 32], F32) for s in range(2)]
            psc = [pst(pes, "psc%d" % s, [128, 512]) for s in range(2)]
            ptk = [pst(pes, "ptk%d" % s, [128, 2048], BF16) for s in range(2)]
            pdt = pst(pes, "pdt", [128, 512])
            bcdg = P.buf("cdg")
            for c in range(12):
                cw = pp[:, PP_CW + (e_ * 12 + c) * 4:PP_CW + (e_ * 12 + c + 1) * 4]
                P.op("dve", lambda e, c=c, cw=cw: e.tensor_tensor(
                    out=cdg[:, c, :, :], in0=ident_f[:].unsqueeze(1).to_broadcast([128, 4, 128]),
                    in1=cw.unsqueeze(2).to_broadcast([128, 4, 128]), op=ALU.mult), r=(bpp,), w=(bcdg,))
            tcnt = 0
            for tl in range(T // 512):
                t0 = tl * 512
                us = tl % 2
                bxh = P.buf("xh", us)
                lo, hi = max(t0 - 1, 0), min(t0 + 514, T)
                if t0 == 0:
                    P.op("pool", lambda e, us=us: e.memset(xh[us][:, :, 0:1], 0.0), w=(bxh,))
                if t0 + 512 == T:
                    P.op("pool", lambda e, us=us: e.memset(xh[us][:, :, 513:515], 0.0), w=(bxh,))
                P.dma("sp", lambda e, us=us, lo=lo, hi=hi, t0=t0: e.dma_start(
                    out=xh[us][:, :, 1 + lo - t0:1 + hi - t0], in_=xbr[:, :, lo:hi]), w=(bxh,), key=bxh)
                if t0 == HALF:
                    P.op("pool", lambda e, us=us: e.tensor_scalar(out=xh[us][:, :, 0:1], in0=xh[us][:, :, 0:1],
                                                                 scalar1=keep_ap, scalar2=None, op0=ALU.mult), r=(bxh,), w=(bxh,))
                if t0 + 512 == HALF:
                    P.op("pool", lambda e, us=us: e.tensor_scalar(out=xh[us][:, :, 513:515], in0=xh[us][:, :, 513:515],
                                                                 scalar1=keep_ap, scalar2=None, op0=ALU.mult), r=(bxh,), w=(bxh,))
                for c in range(12):
                    pc, bpc = psc[c % 2], P.buf("psc", c % 2)
                    for j in range(4):
                        P.op("pe", lambda e, pc=pc, c=c, j=j, us=us: e.matmul(
                            pc[:], cdg[:, c, j, :], xh[us][:, c, j:j + 512], start=(j == 0), stop=(j == 3)),
                            r=(bcdg, bxh), w=(bpc,))
                    bxc = P.buf("xcb", c)
                    P.op("act", lambda e, pc=pc, c=c: e.activation(
                        out=xcb[:, c, :], in_=pc[:], func=AF.Silu,
                        bias=pp[:, PP_CB + e_ * 12 + c:PP_CB + e_ * 12 + c + 1]), r=(bpc, bpp), w=(bxc,))
                P.dma("act", lambda e, t0=t0: e.dma_start(out=xcr[:, :, t0:t0 + 512], in_=xcb[:]),
                      r=tuple(P.buf("xcb", c) for c in range(12)), w=(P.buf("xco", tl),), key=P.buf("xcb", 0))
                bdl = P.buf("dlt", us)
                P.dma("sp", lambda e, us=us, t0=t0: e.dma_start(out=dlt[us][:], in_=del_d[:, t0:t0 + 512]), w=(bdl,), key=bdl)
                for tb in range(4):
                    ki = tcnt % 2
                    tcnt += 1
                    bpk, bxt = P.buf("ptk", ki), P.buf("xtk", ki)
                    for c in range(10):
                        P.op("pe", lambda e, ki=ki, c=c, tb=tb: e.transpose(
                            out=ptk[ki][:, c * 128:(c + 1) * 128], in_=xcb[:, c, tb * 128:(tb + 1) * 128],
                            identity=ident_b[:]), r=(P.buf("xcb", c),), w=(bpk,))
                    P.op("dve", lambda e, ki=ki: e.tensor_copy(out=xtk[ki][:], in_=ptk[ki][:, 0:1280]), r=(bpk,), w=(bxt,))
                    P.dma("act", lambda e, ki=ki, t0=t0, tb=tb: e.dma_start(
                        out=xtok_d[t0 + tb * 128:t0 + (tb + 1) * 128, :], in_=xtk[ki][:]), r=(bxt,),
                        w=(P.buf("xto", t0 // 128 + tb),), key=bxt)
                    bpd, bdk = P.buf("pdt"), P.buf("dtk", ki)
                    P.op("pe", lambda e, us=us, tb=tb: e.transpose(
                        out=pdt[:, 0:32], in_=dlt[us][:, tb * 128:(tb + 1) * 128], identity=ident_f[0:32, 0:32]),
                        r=(bdl,), w=(bpd,))
                    P.op("act", lambda e, ki=ki: e.copy(out=dtk[ki][:], in_=pdt[:, 0:32]), r=(bpd,), w=(bdk,))
                    P.dma("act", lambda e, ki=ki, t0=t0, tb=tb: e.dma_start(
                        out=dtok_d[t0 + tb * 128:t0 + (tb + 1) * 128, :], in_=dtk[ki][:]), r=(bdk,),
                        w=(P.buf("dto", t0 // 128 + tb),), key=bdk)
            P.flush("s0_%d" % e_)

        with ExitStack() as pes:
            A = sb(pes, "A", [128, 32], F32)
            prevb = sb(pes, "prevb", [128, NCH, 1024], BF16)
            prevf = [sb(pes, "prevf%d" % s, [128, 1024], BF16) for s in range(2)]
            H = [sb(pes, "H%d" % s, [128, 1024], F32) for s in range(2)]
            xt = [sb(pes, "xt%d" % s, [128, 1280], BF16) for s in range(2)]
            dt = [sb(pes, "dt%d" % s, [128, 32], F32) for s in range(2)]
            fm = [sb(pes, "fm%d" % s, [128, 12, 128], BF16) for s in range(2)]
            zt = [sb(pes, "zt%d" % s, [128, 8, 128], BF16) for s in range(2)]
            adt = [sb(pes, "adt%d" % s, [128, 32], F32) for s in range(2)]
            wx = [sb(pes, "wx%d" % s, [128, 64], F32) for s in range(2)]
            sc = [sb(pes, "sc%d" % s, [128, 32], F32) for s in range(2)]
            X = [[sb(pes, "X%d_%d" % (s, d), [128, 1024], BF16) for d in range(2)] for s in range(2)]
            Xw = [sb(pes, "Xw%d" % s, [128, 1024], BF16) for s in range(2)]
            R = [sb(pes, "R%d" % s, [128, 1024], F32) for s in range(2)]
            E = [sb(pes, "E%d" % s, [128, 1024], F32) for s in range(2)]
            Ea = [sb(pes, "Ea%d" % s, [128, 1024], F32) for s in range(2)]
            CBr = [sb(pes, "CBr%d" % s, [128, 128], F32) for s in range(2)]
            CBm = [sb(pes, "CBm%d" % s, [128, 128], F32) for s in range(2)]
            MT = [sb(pes, "MT%d" % s, [128, 1024], BF16) for s in range(4)]
            Cs = [sb(pes, "Cs%d" % s, [128, 1024], BF16) for s in range(4)]
            y1 = sb(pes, "y1", [128, 1024], F32)
            y2 = sb(pes, "y2", [128, 1024], F32)
            sqy = sb(pes, "sqy", [128, 1024], F32)
            rsy = sb(pes, "rsy", [128, 256], F32)
            yo = [sb(pes, "yob%d" % s, [128, 1024], BF16) for s in range(2)]
            tmpH = sb(pes, "tmpH", [128, 1024], F32)
            pDE = [pst(pes, "pDE%d" % s, [128, 512]) for s in range(4)]
            py = [pst(pes, "py%d" % s, [128, 512]) for s in range(2)]
            pS = pst(pes, "pSt", [128, 512])
            psm = pst(pes, "psm", [128, 512])
            bA, bH = P.buf("A"), [P.buf("H", 0), P.buf("H", 1)]
            P.dma("sp", lambda e: e.dma_start(out=A[:], in_=alog_d[e_:e_ + 1, :].partition_broadcast(128)), w=(bA,), key=bA)
            P.op("act", lambda e: e.activation(out=A[:], in_=A[:], func=AF.Exp), r=(bA,), w=(bA,))
            P.op("dve", lambda e: e.tensor_scalar(out=A[:], in0=A[:], scalar1=-1.0, scalar2=None, op0=ALU.mult), r=(bA,), w=(bA,))
            P.op("dve", lambda e: e.memset(H[0][:], 0.0), w=(bH[0],))
            P.op("dve", lambda e: e.memset(H[1][:], 0.0), w=(bH[1],))
            bsm = P.buf("psm")
            dsk_bc = pp[:, PP_DSK + e_ * 8:PP_DSK + e_ * 8 + 8].unsqueeze(2).to_broadcast([128, 8, 128])
            sn_bc = pp[:, PP_SN + e_ * 8:PP_SN + e_ * 8 + 8].unsqueeze(2).to_broadcast([128, 8, 128])
            lc = [0]

            def load_chunk(c, full):
                s = lc[0] % 2
                lc[0] += 1
                bxt, bdt = P.buf("xt", s), P.buf("dt", s)
                P.dma("sp", lambda e: e.dma_start(out=xt[s][:], in_=xtok_d[c * 128:(c + 1) * 128, :]), w=(bxt,), key=bxt)
                P.dma("sp", lambda e: e.dma_start(out=dt[s][:], in_=dtok_d[c * 128:(c + 1) * 128, :]), w=(bdt,), key=bdt)
                if full:
                    bfm, bzt = P.buf("fm", s), P.buf("zt", s)
                    P.dma("sp", lambda e: e.dma_start(out=fm[s][:], in_=xcr[:, :, c * 128:(c + 1) * 128]), w=(bfm,), key=bfm)
                    P.dma("sp", lambda e: e.dma_start(out=zt[s][:], in_=zsr[:, :, c * 128:(c + 1) * 128]), w=(bzt,), key=bzt)
                return s

            def decay_small(s, dirs):
                badt, bwx = P.buf("adt", s), P.buf("wx", s)
                P.op("dve", lambda e: e.tensor_tensor(out=adt[s][:], in0=dt[s][:], in1=A[:], op=ALU.mult),
                     r=(P.buf("dt", s), bA), w=(badt,))
                for d in dirs:
                    U = UF if d == 0 else UB
                    P.op("pe", lambda e, d=d, U=U: e.matmul(psm[:, d * 16:(d + 1) * 16], U, adt[s][:, d * 16:(d + 1) * 16],
                                                           start=True, stop=True), r=(badt,), w=(bsm,))
                    P.op("pe", lambda e, d=d: e.matmul(psm[:, 32 + d * 16:48 + d * 16], ones_f[:], adt[s][:, d * 16:(d + 1) * 16],
                                                       start=True, stop=True), r=(badt,), w=(bsm,))
                P.op("act", lambda e: e.activation(out=wx[s][:], in_=psm[:, 0:64], func=AF.Exp), r=(bsm,), w=(bwx,))
                return badt, bwx

            def state_update(s, d, xw_t, bxw, c):
                for g in range(2):
                    bpS = P.buf("pSt")
                    P.op("pe", lambda e, g=g: e.matmul(pS[:], xt[s][:, 1024 + g * 128:1024 + (g + 1) * 128],
                                                       xw_t[:, g * 512:(g + 1) * 512], start=True, stop=True),
                         r=(P.buf("xt", s), bxw), w=(bpS,))
                    hs = slice(g * 512, (g + 1) * 512)
                    dec = wx[s][:, 32 + d * 16 + g * 8:32 + d * 16 + g * 8 + 8].unsqueeze(2).to_broadcast([128, 8, 64])
                    P.op("dve", lambda e, hs=hs, dec=dec: e.tensor_tensor(
                        out=tmpH[:, hs].rearrange("p (h q) -> p h q", h=8), in0=H[d][:, hs].rearrange("p (h q) -> p h q", h=8),
                        in1=dec, op=ALU.mult), r=(bH[d], P.buf("wx", s)), w=(P.buf("tmpH"),))
                    P.op("dve", lambda e, hs=hs: e.tensor_tensor(out=H[d][:, hs], in0=tmpH[:, hs], in1=pS[:], op=ALU.add),
                         r=(P.buf("tmpH"), bpS), w=(bH[d],))

            def xmul(out_t, bout, s, scale_ap, rbufs):
                P.op("dve", lambda e: e.tensor_tensor(
                    out=out_t[:].rearrange("p (h q) -> p h q", h=16), in0=xt[s][:, 0:1024].rearrange("p (h q) -> p h q", h=16),
                    in1=scale_ap.unsqueeze(2).to_broadcast([128, 16, 64]), op=ALU.mult),
                    r=(P.buf("xt", s),) + tuple(rbufs), w=(bout,))

            for c in range(NCH - 1, -1, -1):
                s = load_chunk(c, False)
                badt, bwx = decay_small(s, (1,))
                bsc = P.buf("sc", s)
                P.op("dve", lambda e, s=s: e.tensor_tensor(out=sc[s][:, 16:32], in0=dt[s][:, 16:32], in1=wx[s][:, 16:32], op=ALU.mult),
                     r=(P.buf("dt", s), bwx), w=(bsc,))
                bxw = P.buf("Xw", s)
                xmul(Xw[s], bxw, s, sc[s][:, 16:32], (bsc,))
                if c == NCH // 2 - 1:
                    P.op("dve", lambda e: e.tensor_scalar(out=H[1][:], in0=H[1][:], scalar1=keep_ap, scalar2=None, op0=ALU.mult),
                         r=(bH[1],), w=(bH[1],))
                P.op("pool", lambda e, c=c: e.tensor_copy(out=prevb[:, c, :], in_=H[1][:]), r=(bH[1],), w=(P.buf("prevb", c),))
                state_update(s, 1, Xw[s], bxw, c)
            for c in range(NCH):
                s = load_chunk(c, True)
                bfm, bzt, bxt = P.buf("fm", s), P.buf("zt", s), P.buf("xt", s)
                badt, bwx = decay_small(s, (0, 1))
                bsc = P.buf("sc", s)
                P.op("dve", lambda e, s=s: e.tensor_tensor(out=sc[s][:, 0:16], in0=dt[s][:, 0:16], in1=wx[s][:, 0:16], op=ALU.mult),
                     r=(P.buf("dt", s), bwx), w=(bsc,))
                bxw = P.buf("Xw", s)
                xmul(Xw[s], bxw, s, sc[s][:, 0:16], (bsc,))
                bX = [P.buf("X", s, 0), P.buf("X", s, 1)]
                for d in range(2):
                    xmul(X[s][d], bX[d], s, dt[s][:, d * 16:(d + 1) * 16], (P.buf("dt", s),))
                if c == NCH // 2:
                    P.op("dve", lambda e: e.tensor_scalar(out=H[0][:], in0=H[0][:], scalar1=keep_ap, scalar2=None, op0=ALU.mult),
                         r=(bH[0],), w=(bH[0],))
                pf = prevf[c % 2]
                bpf = P.buf("prevf", c % 2)
                P.op("pool", lambda e, pf=pf: e.tensor_copy(out=pf[:], in_=H[0][:]), r=(bH[0],), w=(bpf,))
                bMT, bCs = {}, {}
                it = 0
                for g in range(2):
                    bcb = P.buf("CBr", g)
                    P.op("pe", lambda e, g=g, s=s: e.matmul(psm[:, 128:256], fm[s][:, 8 + g, :], fm[s][:, 10 + g, :],
                                                             start=True, stop=True), r=(bfm,), w=(bsm,))
                    P.op("act", lambda e, g=g: e.copy(out=CBr[g][:], in_=psm[:, 128:256]), r=(bsm,), w=(bcb,))
                    for d in range(2):
                        U = UF if d == 0 else UB
                        TM = TL if d == 0 else TGE
                        ri = it % 2
                        it += 1
                        bR, bE, bEa = P.buf("R", ri), P.buf("E", ri), P.buf("Ea", ri)
                        P.op("dve", lambda e, ri=ri, TM=TM, d=d, g=g, s=s: e.tensor_tensor(
                            out=R[ri][:].rearrange("p (h l) -> p h l", h=8), in0=TM.unsqueeze(1).to_broadcast([128, 8, 128]),
                            in1=adt[s][:, d * 16 + g * 8:d * 16 + g * 8 + 8].unsqueeze(2).to_broadcast([128, 8, 128]),
                            op=ALU.mult), r=(badt,), w=(bR,))
                        for hf in range(2):
                            P.op("pe", lambda e, U=U, ri=ri, hf=hf: e.matmul(pDE[hf][:], U, R[ri][:, hf * 512:(hf + 1) * 512],
                                                                            start=True, stop=True), r=(bR,), w=(P.buf("pDE", hf),))
                            P.op("pe", lambda e, ri=ri, hf=hf: e.matmul(pDE[2 + hf][:], ones_f[:], R[ri][:, hf * 512:(hf + 1) * 512],
                                                                        start=True, stop=True), r=(bR,), w=(P.buf("pDE", 2 + hf),))
                        for hf in range(2):
                            P.op("act", lambda e, ri=ri, hf=hf: e.activation(out=E[ri][:, hf * 512:(hf + 1) * 512], in_=pDE[hf][:],
                                                                             func=AF.Exp), r=(P.buf("pDE", hf),), w=(bE,))
                            P.op("act", lambda e, ri=ri, hf=hf: e.activation(out=Ea[ri][:, hf * 512:(hf + 1) * 512], in_=pDE[2 + hf][:],
                                                                             func=AF.Exp), r=(P.buf("pDE", 2 + hf),), w=(bEa,))
                        bcm = P.buf("CBm", d)
                        P.op("pool", lambda e, d=d, g=g, TM=TM: e.tensor_tensor(out=CBm[d][:], in0=CBr[g][:], in1=TM, op=ALU.mult),
                             r=(bcb,), w=(bcm,))
                        mi = d * 2 + g
                        bMT[mi], bCs[mi] = P.buf("MT", mi), P.buf("Cs", mi)
                        P.op("dve", lambda e, mi=mi, ri=ri, d=d: e.tensor_tensor(
                            out=MT[mi][:].rearrange("p (h l) -> p h l", h=8), in0=E[ri][:].rearrange("p (h l) -> p h l", h=8),
                            in1=CBm[d][:].unsqueeze(1).to_broadcast([128, 8, 128]), op=ALU.mult), r=(bE, bcm), w=(bMT[mi],))
                        P.op("pool", lambda e, mi=mi, ri=ri, g=g, s=s: e.tensor_tensor(
                            out=Cs[mi][:].rearrange("p (h l) -> p h l", h=8), in0=Ea[ri][:].rearrange("p (h l) -> p h l", h=8),
                            in1=fm[s][:, 10 + g, :].unsqueeze(1).to_broadcast([128, 8, 128]), op=ALU.mult), r=(bEa, bfm), w=(bCs[mi],))
                for pair in range(8):
                    g = pair // 4
                    pyb = py[pair // 4]
                    bpy = P.buf("py", pair // 4)
                    col = (pair % 4) * 128
                    for hh in range(2):
                        h = pair * 2 + hh
                        h8 = h % 8
                        out = pyb[hh * 64:(hh + 1) * 64, col:col + 128]
                        terms = [
                            (X[s][0][:, h * 64:(h + 1) * 64], MT[0 * 2 + g][:, h8 * 128:(h8 + 1) * 128], (bX[0], bMT[g])),
                            (X[s][1][:, h * 64:(h + 1) * 64], MT[1 * 2 + g][:, h8 * 128:(h8 + 1) * 128], (bX[1], bMT[2 + g])),
                            (pf[:, h * 64:(h + 1) * 64], Cs[0 * 2 + g][:, h8 * 128:(h8 + 1) * 128], (bpf, bCs[g])),
                            (prevb[:, c, h * 64:(h + 1) * 64], Cs[1 * 2 + g][:, h8 * 128:(h8 + 1) * 128], (P.buf("prevb", c), bCs[2 + g])),
                        ]
                        for ti, (lh, rh, rb) in enumerate(terms):
                            P.op("pe", lambda e, out=out, lh=lh, rh=rh, ti=ti: e.matmul(
                                out, lh, rh, start=(ti == 0), stop=(ti == 3), skip_group_check=True), r=rb, w=(bpy,))
                by1, by2, bsq, brs = P.buf("y1"), P.buf("y2"), P.buf("sqy"), P.buf("rsy")
                P.op("pool", lambda e, s=s: e.tensor_tensor(out=y1[:].rearrange("p (a l) -> p a l", a=8), in0=fm[s][:, 0:8, :],
                                                            in1=dsk_bc, op=ALU.mult), r=(bfm, bpp), w=(by1,))
                for hf in range(2):
                    P.op("dve", lambda e, hf=hf: e.tensor_tensor(out=y2[:, hf * 512:(hf + 1) * 512], in0=py[hf][:],
                                                                 in1=y1[:, hf * 512:(hf + 1) * 512], op=ALU.add),
                         r=(P.buf("py", hf), by1), w=(by2,))
                P.op("dve", lambda e, s=s: e.tensor_tensor(out=y2[:], in0=y2[:], in1=zt[s][:].rearrange("p a l -> p (a l)"), op=ALU.mult),
                     r=(by2, bzt), w=(by2,))
                P.op("act", lambda e: e.activation(out=sqy[:], in_=y2[:], func=AF.Square), r=(by2,), w=(bsq,))
                for g in range(2):
                    for a in range(4):
                        P.op("pe", lambda e, g=g, a=a: e.matmul(psm[:, 256 + g * 128:256 + (g + 1) * 128], ones_f[:],
                                                                sqy[:, (g * 4 + a) * 128:(g * 4 + a + 1) * 128],
                                                                start=(a == 0), stop=(a == 3)), r=(bsq,), w=(bsm,))
                P.op("act", lambda e: e.activation(out=rsy[:], in_=psm[:, 256:512], func=AF.Sqrt, scale=1.0 / 512, bias=eps_t[:, 0:1]),
                     r=(bsm,), w=(brs,))
                P.op("dve", lambda e: e.reciprocal(out=rsy[:], in_=rsy[:]), r=(brs,), w=(brs,))
                for g in range(2):
                    P.op("dve", lambda e, g=g: e.tensor_tensor(
                        out=y2[:, g * 512:(g + 1) * 512].rearrange("p (a l) -> p a l", a=4),
                        in0=y2[:, g * 512:(g + 1) * 512].rearrange("p (a l) -> p a l", a=4),
                        in1=rsy[:, g * 128:(g + 1) * 128].unsqueeze(1).to_broadcast([128, 4, 128]), op=ALU.mult),
                        r=(by2, brs), w=(by2,))
                oi = c % 2
                byo = P.buf("yob", oi)
                P.op("pool", lambda e, oi=oi: e.tensor_tensor(out=yo[oi][:].rearrange("p (a l) -> p a l", a=8),
                                                              in0=y2[:].rearrange("p (a l) -> p a l", a=8), in1=sn_bc, op=ALU.mult),
                     r=(by2, bpp), w=(byo,))
                P.dma("act", lambda e, oi=oi, c=c: e.dma_start(out=mxr[:, 8:16, c * 128:(c + 1) * 128],
                                                                in_=yo[oi][:].rearrange("p (a l) -> p a l", a=8)),
                      r=(byo,), w=(P.buf("mixs", c),), key=byo)
                state_update(s, 0, Xw[s], bxw, c)
            P.flush("s12_%d" % e_)

    first = [True]

    def fs():
        if first[0]:
            first[0] = False
            return x_in
        return None

    for l in [l_ for _ in range(plan.get("repeat", 1)) for l_ in range(DEPTH)]:
        if l % 2 == 0:
            if any((p, l // 2) in phases for p in ("e1", "e2", "e3", "e4")):
                even_layer(l // 2, fs() if ("e4", l // 2) in phases else (x_in if first[0] else None))
        else:
            if ("odd", l // 2) in phases:
                odd_layer(l // 2, fs())
        if ("ffn", l) in phases:
            ffn_layer(l, fs())

    allx = [b for k, b in P.bufs.items() if k[0] in ("xresd",)]
    P.op("sp", lambda e: e.nop(), r=tuple(allx), w=(P.buf("final"),))
    P.flush("final")
    es.close()
    return nc


def _host_tables():
    k = np.arange(128)
    cst = np.zeros((128, NCST), np.float32)
    cst[:, C_TL:C_TL + 128] = (k[:, None] <= k[None, :])
    cst[:, C_TGE:C_TGE + 128] = (k[:, None] >= k[None, :])
    cst[:, C_UF:C_UF + 128] = (k[:, None] > k[None, :])
    cst[:, C_UB:C_UB + 128] = (k[:, None] < k[None, :])
    c = np.arange(256)
    cst[:, C_MASK:C_MASK + 256] = np.where(np.abs(c[None, :] - 64 - k[:, None]) <= 64, 0.0, NEG)
    prot = np.zeros((128, 32), np.float32)
    for dp in range(16):
        prot[dp + 16, dp] = -1.0
        prot[dp, dp + 16] = 1.0
    cst[:, C_PROT:C_PROT + 32] = prot
    return cst


def _rope_table(seq_len, T):
    pos = (np.arange(T) % seq_len).astype(np.float32)
    inv = (np.float32(500000.0) ** (-np.arange(0, 32, 2, dtype=np.float32) / np.float32(32))).astype(np.float32)
    ang = pos[None, :] * inv[:, None]
    r = np.zeros((32, 2, T), np.float32)
    r[:16, 0] = np.cos(ang)
    r[16:, 0] = np.cos(ang)
    r[:16, 1] = np.sin(ang)
    r[16:, 1] = np.sin(ang)
    return r


def _pp_table(inp):
    pp = np.zeros((128, NPP), np.float32)
    for e in range(2):
        pp[:, PP_QK + e * 2] = inp["q_norm"][e]
        pp[:, PP_QK + e * 2 + 1] = inp["k_norm"][e]
        cw = inp["ssm_conv_w"][e].reshape(4, 12, 128)
        pp[:, PP_CW + e * 48:PP_CW + (e + 1) * 48] = cw.transpose(2, 1, 0).reshape(128, 48)
        pp[:, PP_CB + e * 12:PP_CB + (e + 1) * 12] = inp["ssm_conv_b"][e].reshape(12, 128).T
        pp[:, PP_DSK + e * 8:PP_DSK + (e + 1) * 8] = np.repeat(inp["d_skip"][e], 64).reshape(8, 128).T
        pp[:, PP_SN + e * 8:PP_SN + (e + 1) * 8] = inp["ssm_norm"][e].reshape(8, 128).T
        pp[0:32, PP_DTB + e] = inp["dt_bias"][e].reshape(32)
    for o in range(2):
        pp[:, PP_B1 + o * 32:PP_B1 + (o + 1) * 32] = inp["pw1_b"][o].reshape(32, 128).T
        pp[:, PP_DWB + o * 16:PP_DWB + (o + 1) * 16] = inp["dw_b"][o].reshape(16, 128).T
        pp[:, PP_LNG + o * 16:PP_LNG + (o + 1) * 16] = inp["ln_g"][o].reshape(16, 128).T
        pp[:, PP_LNB + o * 16:PP_LNB + (o + 1) * 16] = inp["ln_b"][o].reshape(16, 128).T
        dw = inp["dw_w"][o].reshape(31, 16, 128)
        pp[:, PP_DWW + o * 496:PP_DWW + (o + 1) * 496] = dw.transpose(2, 1, 0).reshape(128, 496)
    return pp


def full_plan():
    ph = set()
    ws = set()
    for l in range(DEPTH):
        if l % 2 == 0:
            ph |= {("e1", l // 2), ("e2", l // 2), ("e3", l // 2), ("e4", l // 2)}
            ws |= {("w_in", l // 2), ("w_out", l // 2)}
        else:
            ph |= {("odd", l // 2)}
            ws |= {("pw1_w", l // 2), ("pw2_w", l // 2)}
        ph |= {("ffn", l)}
        ws |= {("w_gate", l), ("w_up", l), ("w_down", l)}
    return {"T": T, "phases": ph, "weights": ws}


def core_inputs(inp, plan=None):
    inp = {k: np.asarray(v) for k, v in inp.items()}
    cst0 = _host_tables()
    pp = _pp_table(inp)
    gains = np.concatenate([inp["mix_norm"], inp["conf_norm"], inp["ffn_norm"]], axis=0).astype(np.float32)
    alog = inp["a_log"].reshape(2, 32).astype(np.float32)
    common = {k: np.ascontiguousarray(inp[k], dtype=np.float32) for k in
              ("w_in", "w_out", "pw1_w", "pw2_w", "w_gate", "w_up", "w_down")}
    common.update({"gains": gains, "pw2_b": inp["pw2_b"].astype(np.float32), "pp": pp, "alog": alog})
    xp, xsm = inp["x_prompt"], inp["x_sample"]
    maps = []
    for core in range(NCORES):
        m = dict(common)
        cst = cst0.copy()
        if core < 2:
            m["x_in"] = np.ascontiguousarray(xp[2 * core:2 * core + 2].reshape(T, D))
            cst[:, C_FLAG] = 0.0
            cst[:, C_FLAG + 1] = NEG
            m["rope"] = _rope_table(2048, T)
        else:
            m["x_in"] = np.ascontiguousarray(xsm[core - 2])
            cst[:, C_FLAG] = 1.0
            cst[:, C_FLAG + 1] = 0.0
            m["rope"] = _rope_table(4096, T)
        m["cst"] = cst
        maps.append(m)
    return maps


_NC_CACHE = {}

N_LAUNCH = 8


def layer_plan(j):
    l, part = j // 2, j % 2
    ph, ws = set(), set()
    if part == 0:
        if l % 2 == 0:
            ph |= {("e1", l // 2), ("e2", l // 2), ("e3", l // 2), ("e4", l // 2)}
            ws |= {("w_in", l // 2), ("w_out", l // 2)}
        else:
            ph |= {("odd", l // 2)}
            ws |= {("pw1_w", l // 2), ("pw2_w", l // 2)}
    else:
        ph |= {("ffn", l)}
        ws |= {("w_gate", l), ("w_up", l), ("w_down", l)}
    return {"T": T, "phases": ph, "weights": ws}


def kernel(**inputs):
    maps = core_inputs(inputs)
    if N_LAUNCH == 1:
        if "full" not in _NC_CACHE:
            _NC_CACHE["full"] = build(full_plan())
        nc = _NC_CACHE["full"]
        res = run_bass_kernel_spmd(nc, maps, core_ids=list(range(NCORES)))
    else:
        wnames = ("w_in", "w_out", "pw1_w", "pw2_w", "w_gate", "w_up", "w_down")
        for j in range(2 * DEPTH):
            plan = layer_plan(j)
            if j not in _NC_CACHE:
                _NC_CACHE[j] = build(plan)
            used = {k for (k, _) in plan["weights"]}
            lm = [{k: v for k, v in m.items() if k not in wnames or k in used} for m in maps]
            res = run_bass_kernel_spmd(_NC_CACHE[j], lm, core_ids=list(range(NCORES)))
            for m, r in zip(maps, res.results):
                m["x_in"] = np.ascontiguousarray(np.asarray(r["y"], dtype=np.float32))
    ys = [np.asarray(r["y"], dtype=np.float32) for r in res.results]
    y_prompt = np.concatenate([ys[0].reshape(2, 2048, D), ys[1].reshape(2, 2048, D)], axis=0)
    y_sample = np.stack([ys[2], ys[3]], axis=0)
    return (y_prompt, y_sample)
```
